# Optimizing a Trainium2 kernel written in Bass

```python
import jax, jax.numpy as jnp
from jax import lax
import numpy as np

D_MODEL = 1024
BATCH = 4
SEQ = 8192
DEPTH = 4
DEC_BATCH = 32
DEC_SEQ = 16
PAST_LEN = 2048

CHUNK = 64
Q_BLOCK = 128
SB_BLOCK = 128
A_HEADS = 4
A_NOPE = 64
A_ROPE = 32
A_V = 128
A_KV_RANK = 256
ROPE_THETA = 10000.0
B_HEADS = 4
B_HEAD_DIM = 64
FORGET_BIAS_INIT = 2.0
C_HEADS = 4
C_HEAD_DIM = 64

A_WIDTH = A_HEADS * A_V
B_WIDTH = B_HEADS * B_HEAD_DIM
C_WIDTH = C_HEADS * C_HEAD_DIM
D_MIX = A_WIDTH + B_WIDTH + C_WIDTH
A_SCALE = (A_NOPE + A_ROPE) ** -0.5
B_SCALE = B_HEAD_DIM ** -0.5
C_SCALE = C_HEAD_DIM ** -0.5
IN_SIZES = (A_HEADS * A_NOPE, A_HEADS * A_ROPE, A_KV_RANK, A_ROPE,
            B_WIDTH, B_WIDTH, B_WIDTH, B_HEADS,
            C_WIDTH, C_WIDTH, C_WIDTH, D_MIX)
IN_COLS = int(sum(IN_SIZES))
IN_SPLITS = tuple(int(v) for v in np.cumsum(IN_SIZES)[:-1])
GROUP_SPLITS = (A_WIDTH, A_WIDTH + B_WIDTH)
NORM_EPS = 1e-6
NEG_INF = -1e30

kernel_name = 'hybrid_mla_fox_stickbreak_stream_step'


def rmsnorm(x, g):
    xf = x.astype(jnp.float32)
    y = xf * lax.rsqrt(jnp.mean(xf * xf, axis=-1, keepdims=True) + NORM_EPS)
    return (y * g.astype(jnp.float32)).astype(x.dtype)


def rope(x, pos):
    half = x.shape[-1] // 2
    inv = ROPE_THETA ** (-jnp.arange(half, dtype=jnp.float32) / half)
    ang = pos.astype(jnp.float32)[:, None] * inv[None, :]
    ang = ang.reshape((ang.shape[0],) + (1,) * (x.ndim - 3) + (half,))
    cos, sin = jnp.cos(ang), jnp.sin(ang)
    xf = x.astype(jnp.float32)
    x1, x2 = xf[..., :half], xf[..., half:]
    return jnp.concatenate([x1 * cos - x2 * sin, x1 * sin + x2 * cos], axis=-1).astype(x.dtype)


def exclusive_suffix_sum(lk):
    L = lk.shape[-1]
    nk = -(-L // SB_BLOCK)
    pad = nk * SB_BLOCK - L
    lkp = jnp.pad(lk, ((0, 0), (0, 0), (0, 0), (0, pad))).reshape(lk.shape[:3] + (nk, SB_BLOCK))
    idx = jnp.arange(SB_BLOCK)
    tri = (idx[:, None] > idx[None, :]).astype(lk.dtype)
    within = jnp.einsum('bhqnj,js->bhqns', lkp, tri)
    bidx = jnp.arange(nk)
    btri = (bidx[:, None] > bidx[None, :]).astype(lk.dtype)
    later = jnp.einsum('bhqm,mn->bhqn', jnp.sum(lkp, axis=-1), btri)
    return (within + later[..., None]).reshape(lk.shape[:3] + (nk * SB_BLOCK,))[..., :L]


def project(h, pos, w_in, kv_norm, forget_bias):
    B, T, _ = h.shape
    (a_qn, a_qr, a_ckv, a_kr, b_q, b_k, b_v, b_f,
     c_q, c_k, c_v, gate) = jnp.split(h @ w_in, IN_SPLITS, axis=-1)
    a_qn = a_qn.reshape(B, T, A_HEADS, A_NOPE)
    a_qr = rope(a_qr.reshape(B, T, A_HEADS, A_ROPE), pos)
    a_ckv = rmsnorm(a_ckv, kv_norm)
    a_kr = rope(a_kr, pos)
    b_q = b_q.reshape(B, T, B_HEADS, B_HEAD_DIM)
    b_k = b_k.reshape(B, T, B_HEADS, B_HEAD_DIM)
    b_v = b_v.reshape(B, T, B_HEADS, B_HEAD_DIM)
    b_logf = jax.nn.log_sigmoid(b_f.astype(jnp.float32) + forget_bias.astype(jnp.float32))
    c_q = c_q.reshape(B, T, C_HEADS, C_HEAD_DIM)
    c_k = c_k.reshape(B, T, C_HEADS, C_HEAD_DIM)
    c_v = c_v.reshape(B, T, C_HEADS, C_HEAD_DIM)
    return (a_qn, a_qr, a_ckv, a_kr, b_q, b_k, b_v, b_logf, c_q, c_k, c_v, gate)


def attend_block(queries, keys, q_pos, k_pos):
    a_qn, a_qr, b_q, b_Fq, c_q = queries
    a_k, a_kr, a_v, b_k, b_v, b_Fk, c_k, c_v = keys
    B, Tq = a_qn.shape[0], a_qn.shape[1]
    s = (jnp.einsum('bqhd,bkhd->bhqk', a_qn, a_k)
         + jnp.einsum('bqhr,bkr->bhqk', a_qr, a_kr)).astype(jnp.float32)
    chunk_ok = (k_pos[None, :] // CHUNK) <= (q_pos[:, None] // CHUNK)
    p = jax.nn.softmax(jnp.where(chunk_ok, s, NEG_INF), axis=-1)
    o_a = jnp.einsum('bhqk,bkhd->bqhd', p.astype(a_v.dtype), a_v)
    causal = k_pos[None, :] <= q_pos[:, None]
    decay = jnp.transpose(b_Fq, (0, 2, 1))[:, :, :, None] - jnp.transpose(b_Fk, (0, 2, 1))[:, :, None, :]
    s = jnp.einsum('bqhd,bkhd->bhqk', b_q, b_k).astype(jnp.float32) + decay
    p = jax.nn.softmax(jnp.where(causal, s, NEG_INF), axis=-1)
    o_b = jnp.einsum('bhqk,bkhd->bqhd', p.astype(b_v.dtype), b_v)
    strict = k_pos[None, :] < q_pos[:, None]
    z = jnp.einsum('bqhd,bkhd->bhqk', c_q, c_k).astype(jnp.float32)
    log_keep = jnp.where(strict, jax.nn.log_sigmoid(-z), 0.0)
    log_after = exclusive_suffix_sum(log_keep)
    w = jnp.exp(jnp.where(strict, z + log_keep + log_after, NEG_INF))
    o_c = jnp.einsum('bhqk,bkhd->bqhd', w.astype(c_v.dtype), c_v)
    return jnp.concatenate([o_a.reshape(B, Tq, A_WIDTH), o_b.reshape(B, Tq, B_WIDTH),
                            o_c.reshape(B, Tq, C_WIDTH)], axis=-1)


def merge_groups(o, gate, out_norm, w_out):
    oa, ob, oc = jnp.split(o, GROUP_SPLITS, axis=-1)
    ga, gb, gc = jnp.split(out_norm, GROUP_SPLITS, axis=-1)
    normed = jnp.concatenate([rmsnorm(oa, ga), rmsnorm(ob, gb), rmsnorm(oc, gc)], axis=-1)
    return (normed * jax.nn.silu(gate)) @ w_out


def layer(x, pos, past, norm_pre, norm_post, w_in, kv_norm, w_uk, w_uv, forget_bias, out_norm, w_out):
    B, T, _ = x.shape
    h = rmsnorm(x, norm_pre)
    (a_qn, a_qr, a_ckv, a_kr, b_q, b_k, b_v, b_logf,
     c_q, c_k, c_v, gate) = project(h, pos, w_in, kv_norm, forget_bias)
    rows = (a_ckv, a_kr, b_k, b_v, b_logf, c_k, c_v)
    if past is None:
        full = rows
        k_pos = pos
    else:
        full = tuple(jnp.concatenate([pc.astype(r.dtype), r], axis=1) for pc, r in zip(past, rows))
        k_pos = jnp.arange(full[0].shape[1], dtype=jnp.int32)
    f_ckv, f_kr, f_bk, f_bv, f_logf, f_ck, f_cv = full
    a_k = jnp.einsum('bkr,rhd->bkhd', f_ckv, w_uk)
    a_v = jnp.einsum('bkr,rhd->bkhd', f_ckv, w_uv)
    F = lax.cumsum(f_logf.astype(jnp.float32), axis=1)
    Fq = F[:, F.shape[1] - T:]
    keys = (a_k, f_kr, a_v, f_bk, f_bv, F, f_ck, f_cv)
    queries = (a_qn * A_SCALE, a_qr * A_SCALE, b_q * B_SCALE, Fq, c_q * C_SCALE)
    if past is None:
        outs = []
        for i in range(T // Q_BLOCK):
            q0, q1 = i * Q_BLOCK, (i + 1) * Q_BLOCK
            outs.append(attend_block(tuple(t[:, q0:q1] for t in queries),
                                     tuple(t[:, :q1] for t in keys), pos[q0:q1], k_pos[:q1]))
        o = jnp.concatenate(outs, axis=1)
    else:
        o = attend_block(queries, keys, pos, k_pos)
    y = merge_groups(o, gate, out_norm, w_out)
    return x + rmsnorm(y, norm_post), rows


def setup_inputs(seed: int = 0) -> dict:
    key = jax.random.key(seed)
    ks = jax.random.split(key, 20)
    f32 = jnp.float32
    nrm = lambda k, shape, scale=1.0: scale * jax.random.normal(k, shape, f32)
    return {
        'x_prompt': nrm(ks[0], (BATCH, SEQ, D_MODEL)),
        'x_sample': nrm(ks[1], (DEC_BATCH, DEC_SEQ, D_MODEL)),
        'cache_mla_ckv': nrm(ks[2], (DEPTH, DEC_BATCH, PAST_LEN, A_KV_RANK)),
        'cache_mla_kpe': nrm(ks[3], (DEPTH, DEC_BATCH, PAST_LEN, A_ROPE)),
        'cache_fox_k': nrm(ks[4], (DEPTH, DEC_BATCH, PAST_LEN, B_HEADS, B_HEAD_DIM)),
        'cache_fox_v': nrm(ks[5], (DEPTH, DEC_BATCH, PAST_LEN, B_HEADS, B_HEAD_DIM)),
        'cache_fox_logf': jax.nn.log_sigmoid(FORGET_BIAS_INIT + nrm(ks[6], (DEPTH, DEC_BATCH, PAST_LEN, B_HEADS))),
        'cache_sb_k': nrm(ks[7], (DEPTH, DEC_BATCH, PAST_LEN, C_HEADS, C_HEAD_DIM)),
        'cache_sb_v': nrm(ks[8], (DEPTH, DEC_BATCH, PAST_LEN, C_HEADS, C_HEAD_DIM)),
        'norm_pre': 1.0 + nrm(ks[9], (DEPTH, D_MODEL), 0.05),
        'norm_post': 1.0 + nrm(ks[10], (DEPTH, D_MODEL), 0.05),
        'w_in': nrm(ks[11], (DEPTH, D_MODEL, IN_COLS), D_MODEL ** -0.5),
        'mla_kv_norm': 1.0 + nrm(ks[12], (DEPTH, A_KV_RANK), 0.05),
        'mla_w_uk': nrm(ks[13], (DEPTH, A_KV_RANK, A_HEADS, A_NOPE), A_KV_RANK ** -0.5),
        'mla_w_uv': nrm(ks[14], (DEPTH, A_KV_RANK, A_HEADS, A_V), A_KV_RANK ** -0.5),
        'fox_forget_bias': FORGET_BIAS_INIT + nrm(ks[15], (DEPTH, B_HEADS), 0.1),
        'out_norm': 1.0 + nrm(ks[16], (DEPTH, D_MIX), 0.05),
        'w_out': nrm(ks[17], (DEPTH, D_MIX, D_MODEL), D_MIX ** -0.5),
    }


def reference(x_prompt, x_sample, cache_mla_ckv, cache_mla_kpe, cache_fox_k, cache_fox_v,
              cache_fox_logf, cache_sb_k, cache_sb_v, norm_pre, norm_post, w_in, mla_kv_norm,
              mla_w_uk, mla_w_uv, fox_forget_bias, out_norm, w_out):
    past_len = cache_mla_ckv.shape[2]
    pos_p = jnp.arange(x_prompt.shape[1], dtype=jnp.int32)
    pos_s = past_len + jnp.arange(x_sample.shape[1], dtype=jnp.int32)
    xp, xs = x_prompt, x_sample
    rows_p, rows_s = [], []
    for l in range(DEPTH):
        wl = (norm_pre[l], norm_post[l], w_in[l], mla_kv_norm[l], mla_w_uk[l], mla_w_uv[l],
              fox_forget_bias[l], out_norm[l], w_out[l])
        xp, rp = layer(xp, pos_p, None, *wl)
        past = (cache_mla_ckv[l], cache_mla_kpe[l], cache_fox_k[l], cache_fox_v[l],
                cache_fox_logf[l], cache_sb_k[l], cache_sb_v[l])
        xs, rs = layer(xs, pos_s, past, *wl)
        rows_p.append(rp)
        rows_s.append(rs)
    st = lambda rows, i: jnp.stack([r[i] for r in rows], axis=0)
    return (xp, xs,
            st(rows_p, 0), st(rows_p, 1), st(rows_p, 2), st(rows_p, 3), st(rows_p, 4), st(rows_p, 5), st(rows_p, 6),
            st(rows_s, 0), st(rows_s, 1), st(rows_s, 2), st(rows_s, 3), st(rows_s, 4), st(rows_s, 5), st(rows_s, 6))
```

```python
import math
from contextlib import ExitStack

import numpy as np
import concourse.bass as bass
import concourse.mybir as mybir
from concourse.bass_utils import run_bass_kernel_spmd

F32 = mybir.dt.float32
BF16 = mybir.dt.bfloat16
AF = mybir.ActivationFunctionType
ALU = mybir.AluOpType

D = 1024
A_SCALE = 96.0 ** -0.5
B_SCALE = 64.0 ** -0.5
C_SCALE = 64.0 ** -0.5
EPS = 1e-6
NEG = -30000.0
DS = 16
QR, CKV, KR, BF, BK, BV, CK, CV = (0, 128), (128, 384), (384, 416), (416, 420), (420, 676), (676, 932), (932, 1188), (1188, 1444)
NT_COLS = 1444
NF_COLS = 1792
DK = {"A": 96, "B": 70, "C": 96}
DV = {"A": 128, "B": 64, "C": 64}


class Op:
    __slots__ = ("eng", "fn", "deps", "signal", "dsem", "tok", "idx")

    def __init__(self, eng, fn, dsem):
        self.eng = eng
        self.fn = fn
        self.deps = []
        self.signal = False
        self.dsem = dsem
        self.tok = None
        self.idx = 0


class Sched:
    ENGS = ("pe", "act", "dve", "pool", "sp")

    def __init__(self):
        self.ops = {e: [] for e in self.ENGS}
        self.res = {}
        self.dma_keys = []
        self.n = 0
        self.final_dma = {}

    def add(self, eng, fn, reads=(), writes=(), dsem=None):
        op = Op(eng, fn, dsem)
        op.idx = self.n
        self.n += 1
        if dsem is not None and dsem not in self.dma_keys:
            self.dma_keys.append(dsem)
        deps = {}
        for r in reads:
            st = self.res.get(r)
            if st is None:
                st = self.res[r] = [None, []]
            if st[0] is not None:
                deps[id(st[0])] = st[0]
            st[1].append(op)
        for w in writes:
            st = self.res.get(w)
            if st is None:
                st = self.res[w] = [None, []]
            if st[0] is not None:
                deps[id(st[0])] = st[0]
            for rd in st[1]:
                if rd is not op:
                    deps[id(rd)] = rd
            st[0] = op
            st[1] = []
        for d in deps.values():
            if d is op:
                continue
            if d.dsem is None and op.dsem is None and d.eng == "pe" and op.eng == "pe":
                continue
            if d.dsem is not None and op.dsem is not None and d.dsem == op.dsem and d.eng == op.eng:
                continue
            d.signal = True
            op.deps.append(d)
        self.ops[eng].append(op)
        return op

    def barrier(self, eng="sp"):
        op = Op(eng, None, None)
        op.idx = self.n
        self.n += 1
        last = {}
        for e in self.ENGS:
            for o in self.ops[e]:
                if o.dsem is not None and (o.dsem not in last or o.idx > last[o.dsem].idx):
                    last[o.dsem] = o
        op.deps = list(last.values())
        self.ops[eng].append(op)

    def build(self):
        for e in self.ENGS:
            c = 0
            for op in self.ops[e]:
                if op.dsem is None and op.signal:
                    c += 1
                    op.tok = (("E", e), c)
        dcount = {}
        order = sorted((op for e in self.ENGS for op in self.ops[e] if op.dsem is not None), key=lambda o: o.idx)
        for op in order:
            dcount[op.dsem] = dcount.get(op.dsem, 0) + 16
            op.tok = (("D", op.dsem), dcount[op.dsem])
        self.final_dma = dict(dcount)

    def emit(self, e, eng, sems):
        known = {}
        for op in self.ops[e]:
            need = {}
            for d in op.deps:
                k, v = d.tok
                if v > need.get(k, 0):
                    need[k] = v
            for k, v in need.items():
                if known.get(k, 0) >= v:
                    continue
                eng.wait_ge(sems[k], v)
                known[k] = v
            if op.fn is None:
                continue
            ins = op.fn(eng)
            if op.dsem is not None:
                ins.then_inc(sems[("D", op.dsem)], 16)
            elif op.signal:
                ins.then_inc(sems[("E", e)], 1)
        if e == "sp":
            for k, v in self.final_dma.items():
                eng.wait_ge(sems[("D", k)], v)


class Stream:
    pass


def build(SEQ, PAST, DEPTH, NSB, NQP=512):
    nc = bass.Bass("TRN2", target_bir_lowering=False)
    S = Sched()
    NKS = PAST + DS
    NBP = SEQ // 128
    NBS = PAST // 128 + 1
    TS = NSB * DS

    def din(name, shape):
        return nc.dram_tensor(name, list(shape), F32, kind="ExternalInput").ap()

    def dout(name, shape):
        return nc.dram_tensor(name, list(shape), F32, kind="ExternalOutput").ap()

    def dint(name, shape, dt=BF16):
        return nc.dram_tensor(name, list(shape), dt, kind="Internal").ap()

    xp = din("xp", [SEQ, D])
    xs = din("xs", [TS, D])
    c_in = {
        "ckv": din("c_ckv", [DEPTH, NSB, PAST, 256]), "kpe": din("c_kpe", [DEPTH, NSB, PAST, 32]),
        "fk": din("c_fk", [DEPTH, NSB, PAST, 256]), "fv": din("c_fv", [DEPTH, NSB, PAST, 256]),
        "lf": din("c_lf", [DEPTH, NSB, PAST, 4]), "sk": din("c_sk", [DEPTH, NSB, PAST, 256]),
        "sv": din("c_sv", [DEPTH, NSB, PAST, 256]),
    }
    wf_d = din("wf", [DEPTH, D, NF_COLS])
    wt_d = din("wt", [DEPTH, D, NT_COLS])
    wuk_d = din("wuk", [DEPTH, 256, 256])
    wuv_d = din("wuv", [DEPTH, 256, 512])
    wo_d = din("wo", [DEPTH, D, D])
    npre_d = din("npre", [DEPTH, D])
    npost_d = din("npost", [DEPTH, D])
    kvn_d = din("kvn", [DEPTH, 256])
    fb_d = din("fb", [DEPTH, 4])
    onorm_d = din("onorm", [DEPTH, D])
    ident_d = din("ident", [128, 128])
    trin_d = din("trin", [2, 128, 128])
    masks_d = din("masks", [12, 128, 512])
    cosp_d = din("cosp", [SEQ, 64])
    sinp_d = din("sinp", [SEQ, 64])
    coss_d = din("coss", [DS, 64])
    sins_d = din("sins", [DS, 64])

    yp = dout("yp", [SEQ, D])
    ys = dout("ys", [TS, D])
    outp = {"ckv": dout("p_ckv", [DEPTH, SEQ, 256]), "kpe": dout("p_kpe", [DEPTH, SEQ, 32]),
            "fk": dout("p_fk", [DEPTH, SEQ, 256]), "fv": dout("p_fv", [DEPTH, SEQ, 256]),
            "lf": dout("p_lf", [DEPTH, SEQ, 4]), "sk": dout("p_sk", [DEPTH, SEQ, 256]),
            "sv": dout("p_sv", [DEPTH, SEQ, 256])}
    outs = {"ckv": dout("s_ckv", [DEPTH, TS, 256]), "kpe": dout("s_kpe", [DEPTH, TS, 32]),
            "fk": dout("s_fk", [DEPTH, TS, 256]), "fv": dout("s_fv", [DEPTH, TS, 256]),
            "lf": dout("s_lf", [DEPTH, TS, 4]), "sk": dout("s_sk", [DEPTH, TS, 256]),
            "sv": dout("s_sv", [DEPTH, TS, 256])}
    xbuf_p = dint("xbuf_p", [SEQ, D], F32)
    xbuf_s = dint("xbuf_s", [TS, D], F32)

    def mkstream(name, T, NK, NB, key_off):
        st = Stream()
        st.name, st.T, st.NK, st.NB, st.key_off = name, T, NK, NB, key_off
        st.Q = {"A": dint(name + "_QA", [4, 96, T]), "B": dint(name + "_QB", [4, 70, T]), "C": dint(name + "_QC", [4, 96, T])}
        st.G = {"A": dint(name + "_GA", [4, 128, T]), "B": dint(name + "_GB", [4, 64, T]), "C": dint(name + "_GC", [4, 64, T])}
        st.KA = dint(name + "_KA", [4, 64, NK])
        st.KPE = dint(name + "_KPE", [32, NK])
        st.KB = dint(name + "_KB", [4, 70, NK])
        st.KC = dint(name + "_KC", [4, 96, NK])
        st.V = {"A": dint(name + "_VA", [128, NB, 4, 128]), "B": dint(name + "_VB", [128, NB, 4, 64]),
                "C": dint(name + "_VC", [128, NB, 4, 64])}
        return st

    stp = mkstream("P", SEQ, SEQ, NBP, 0)
    sts = [mkstream(f"S{b}", DS, NKS, NBS, PAST) for b in range(NSB)]

    es = ExitStack()
    with es:
        def sb(name, shape, dt=F32):
            return es.enter_context(nc.sbuf_tensor("sb_" + name, list(shape), dt))

        ps = [es.enter_context(nc.psum_tensor(f"ps{i}", [128, 512], F32)) for i in range(8)]
        PSN = [f"ps{i}" for i in range(8)]

        ident = sb("ident", [128, 128])
        ones_f = sb("ones_f", [128, 128])
        ones_b = sb("ones_b", [128, 128], BF16)
        trin_b = sb("trin_b", [128, 2, 128], BF16)
        ident_b = sb("ident_b", [128, 128], BF16)
        masks = sb("masks", [128, 12, 512], BF16)
        wstage = None
        Wf = sb("Wf", [128, 8, NF_COLS], BF16)
        Wt = sb("Wt", [128, 8, NT_COLS], BF16)
        Wuk = sb("Wuk", [128, 2, 256], BF16)
        Wuv = sb("Wuv", [128, 2, 512], BF16)
        WoA = sb("WoA", [128, 4, D], BF16)
        WoB = sb("WoB", [128, 4, D], BF16)
        WoC = sb("WoC", [128, 4, D], BF16)
        gpre = sb("gpre", [128, 8])
        gpost = sb("gpost", [128, D])
        kvn = sb("kvn", [128, 256])
        fbb = sb("fbb", [128, 4])
        gout = {"A": sb("goutA", [128, 4]), "B": sb("goutB", [64, 4]), "C": sb("goutC", [64, 4])}
        xsl = [sb(f"xsl{i}", [128, D]) for i in range(2)]
        hsl = sb("hsl", [128, D])
        junk = None
        small = sb("small", [128, 16])
        hT = sb("hT", [128, 8, 512], BF16)
        fev = [sb(f"fev{i}", [128, 512], BF16) for i in range(2)]
        rows = [sb(f"rows{i}", [128, NT_COLS]) for i in range(2)]
        ropet = sb("ropet", [128, 4, 64])
        ropetmp = sb("ropetmp", [128, 4, 64])
        tA = sb("tA", [128, 4, 128], BF16)
        tB = sb("tB", [128, 4, 128], BF16)
        tK = sb("tK", [128, 2, 128], BF16)
        tV = sb("tV", [128, 512], BF16)
        junk = tV
        tVB = sb("tVB", [128, 256], BF16)
        tVC = sb("tVC", [128, 256], BF16)
        lfT = sb("lfT", [4, 128])
        Ft = sb("Ft", [4, 128])
        Fr = lfT
        Fq3 = sb("Fq3", [4, 3, 128], BF16)
        Fk3 = sb("Fk3", [4, 3, 128], BF16)
        onesF = sb("onesF", [4, 128])
        carry = sb("carry", [4, 1 + NSB])
        ones3 = sb("ones3", [4, 3, 128], BF16)
        qt = [sb(f"qt{i}", [96, NQP], BF16) for i in range(4)]
        KCH = 4
        kt = [sb(f"kt{i}", [96, KCH * 128], BF16) for i in range(4)]
        vt = [sb(f"vt{i}", [128, KCH, 128], BF16) for i in range(4)]
        pt = [sb(f"pt{i}", [128, NQP], BF16) for i in range(4)]
        et = [sb(f"et{i}", [128, NQP]) for i in range(3)]
        wstage = et
        spt = [sb(f"spt{i}", [128, NQP], BF16) for i in range(3)]
        argt = [sb(f"argt{i}", [128, NQP]) for i in range(2)]
        wt_ = [sb(f"wt{i}", [128, NQP], BF16) for i in range(2)]
        rden = None
        rbt = None
        Oall = sb("Oall", [128, 4, NQP])
        OCt = sb("OCt", [128, 4, NQP])
        Og = {"A": Oall, "B": Oall, "C": OCt}
        sqt = sb("sqt", [128, NQP])
        lnt = sb("lnt", [128, NQP])
        rden = lnt
        rbt = sqt
        rst = lnt
        tmpm = sqt
        gtall = sb("gtall", [128, 4, NQP], BF16)
        gtile = {"A": gtall, "B": gtall, "C": gtall}
        Mg = {"A": sb("MA", [128, 4, NQP], BF16), "B": sb("MB", [128, 4, NQP], BF16), "C": sb("MC", [128, 4, NQP], BF16)}
        xres, xnew = xsl[0], xsl[1]

        cnt = {"ld": 0, "hd": 0, "q0": 0, "q1": 0, "k0": 0, "k1": 0}

        def dma(out, in_, reads, writes, key, eng="sp", slow=False):
            if slow:
                S.add(eng, lambda e: e.dma_start(out=out, in_=in_, allow_slow_non_contiguous=True), reads, writes, dsem=key)
            else:
                S.add(eng, lambda e: e.dma_start(out=out, in_=in_), reads, writes, dsem=key)

        def mm(out, lhsT, rhs, start, stop, reads, writes, skip=False):
            if skip:
                S.add("pe", lambda e: e.matmul(out, lhsT=lhsT, rhs=rhs, start=start, stop=stop, skip_group_check=True), reads, writes)
            else:
                S.add("pe", lambda e: e.matmul(out, lhsT=lhsT, rhs=rhs, start=start, stop=stop), reads, writes)

        def tr(out, in_, n, reads, writes):
            S.add("pe", lambda e: e.transpose(out=out, in_=in_, identity=ident[0:n, 0:n]), list(reads) + ["ident"], writes)

        def act(out, in_, func, reads, writes, **kw):
            S.add("act", lambda e: e.activation(out=out, in_=in_, func=func, **kw), reads, writes)

        def dve(name, reads, writes, *a, **kw):
            S.add("dve", lambda e: getattr(e, name)(*a, **kw), reads, writes)

        def pool(name, reads, writes, *a, **kw):
            S.add("pool", lambda e: getattr(e, name)(*a, **kw), reads, writes)

        dma(ident[:, :], ident_d, [], ["ident"], "ld_ident")
        pool("memset", [], ["ones_f"], ones_f[:, :], 1.0)
        pool("memset", [], ["ones_b"], ones_b[:, :], 1.0)
        pool("memset", [], ["onesF"], onesF[:, :], 1.0)
        pool("memset", [], ["ones3"], ones3[:, :, :], 1.0)
        for j in range(2):
            dma(wstage[j][:, 0:128], trin_d[j], [], [f"et{j}"], f"ld_ws{j}")
            pool("tensor_copy", [f"et{j}"], ["trin_b"], out=trin_b[:, j, :], in_=wstage[j][:, 0:128])
        pool("tensor_copy", ["ident"], ["ident_b"], out=ident_b[:, :], in_=ident[:, :])
        for i in range(12):
            s = i % 2
            dma(wstage[s][:, 0:512], masks_d[i], [], [f"et{s}"], f"ld_ws{s}")
            pool("tensor_copy", [f"et{s}"], ["masks"], out=masks[:, i, :], in_=wstage[s][:, 0:512])
        pool("memset", [], ["fev0"], fev[0][:, :], 0.0)
        for i in range(4):
            pool("memset", [], [f"vt{i}"], vt[i][:, :, :], 0.0)
        pool("memset", [], ["MB"], Mg["B"][:, :, :], 0.0)
        pool("memset", [], ["MC"], Mg["C"][:, :, :], 0.0)
        pool("memset", [], ["WoB"], WoB[:, :, :], 0.0)
        pool("memset", [], ["WoC"], WoC[:, :, :], 0.0)
        for st in [stp] + sts:
            for h in range(4):
                for t0 in range(0, st.T, 512):
                    n = min(512, st.T - t0)
                    dma(st.Q["C"][h, 64:96, t0:t0 + n], fev[0][0:32, 0:n], ["fev0"], [(st.name, "QBones")], "st_ones")
                for t0 in range(0, st.NK, 512):
                    n = min(512, st.NK - t0)
                    dma(st.KC[h, 64:96, t0:t0 + n], fev[0][0:32, 0:n], ["fev0"], [(st.name, "KBones")], "st_ones")
        for st in [stp] + sts:
            for t0 in range(0, st.T, 128):
                n = min(128, st.T - t0)
                dma(st.Q["B"][:, 67:70, t0:t0 + n], ones3[:, :, 0:n], ["ones3"], [(st.name, "QBones")], "st_ones")
            for t0 in range(0, st.NK, 128):
                n = min(128, st.NK - t0)
                dma(st.KB[:, 64:67, t0:t0 + n], ones3[:, :, 0:n], ["ones3"], [(st.name, "KBones")], "st_ones")

        wsi = {"i": 0}

        def load_w(dst_ap, src_ap, npart, ncols, dstname):
            c0 = 0
            while c0 < ncols:
                c1 = min(ncols, c0 + 512)
                s = wsi["i"] % 2
                wsi["i"] += 1
                dma(wstage[s][0:npart, 0:c1 - c0], src_ap[:, c0:c1], [], [f"et{s}"], f"ld_ws{s}")
                pool("tensor_copy", [f"et{s}"], [dstname], out=dst_ap[:, c0:c1], in_=wstage[s][0:npart, 0:c1 - c0])
                c0 = c1

        def load_layer(l):
            for kc in range(8):
                load_w(Wf[:, kc, :], wf_d[l, kc * 128:(kc + 1) * 128, :], 128, NF_COLS, "Wf")
                load_w(Wt[:, kc, :], wt_d[l, kc * 128:(kc + 1) * 128, :], 128, NT_COLS, "Wt")
            for rc in range(2):
                load_w(Wuk[:, rc, :], wuk_d[l, rc * 128:(rc + 1) * 128, :], 128, 256, "Wuk")
                load_w(Wuv[:, rc, :], wuv_d[l, rc * 128:(rc + 1) * 128, :], 128, 512, "Wuv")
            for h in range(4):
                load_w(WoA[:, h, :], wo_d[l, h * 128:(h + 1) * 128, :], 128, D, "WoA")
                load_w(WoB[0:64, h, :], wo_d[l, 512 + h * 64:512 + (h + 1) * 64, :], 64, D, "WoB")
                load_w(WoC[0:64, h, :], wo_d[l, 768 + h * 64:768 + (h + 1) * 64, :], 64, D, "WoC")
            dma(gpre[:, :], npre_d[l, :].rearrange("(k p) -> p k", p=128), [], ["gpre"], "ld_small", slow=True)
            dma(gpost[:, :], npost_d[l:l + 1, :].partition_broadcast(128), [], ["gpost"], "ld_small")
            dma(kvn[:, :], kvn_d[l:l + 1, :].partition_broadcast(128), [], ["kvn"], "ld_small")
            dma(fbb[:, :], fb_d[l:l + 1, :].partition_broadcast(128), [], ["fbb"], "ld_small")
            dma(gout["A"][:, :], onorm_d[l, 0:512].rearrange("(h p) -> p h", p=128), [], ["gout"], "ld_small", slow=True)
            dma(gout["B"][:, :], onorm_d[l, 512:768].rearrange("(h p) -> p h", p=64), [], ["gout"], "ld_small", slow=True)
            dma(gout["C"][:, :], onorm_d[l, 768:1024].rearrange("(h p) -> p h", p=64), [], ["gout"], "ld_small", slow=True)

        def rstd_from_ss(ss_ap, out_ap, n, width, rd, wr):
            act(out_ap, ss_ap, AF.Ln, rd, wr, scale=1.0 / width, bias=EPS)
            act(out_ap, out_ap, AF.Exp, wr, wr, scale=-0.5)

        def rope_inplace(rw, rname, n, c0, nh, cs_key):
            x = rw[0:n, c0:c0 + nh * 32].rearrange("p (h t i) -> p h t i", h=nh, t=2)
            cosv = ropet[0:n, 0, 0:nh * 16].rearrange("p (h i) -> p h i", h=nh)
            sinv = ropet[0:n, 1, 0:nh * 16].rearrange("p (h i) -> p h i", h=nh)
            t = [ropetmp[0:n, j, 0:nh * 16].rearrange("p (h i) -> p h i", h=nh) for j in range(4)]
            x1, x2 = x[:, :, 0, :], x[:, :, 1, :]
            dve("tensor_tensor", [rname, cs_key], ["ropetmp"], out=t[0], in0=x1, in1=cosv, op=ALU.mult)
            dve("tensor_tensor", [rname, cs_key], ["ropetmp"], out=t[1], in0=x2, in1=sinv, op=ALU.mult)
            dve("tensor_tensor", [rname, cs_key], ["ropetmp"], out=t[2], in0=x1, in1=sinv, op=ALU.mult)
            dve("tensor_tensor", [rname, cs_key], ["ropetmp"], out=t[3], in0=x2, in1=cosv, op=ALU.mult)
            dve("tensor_tensor", ["ropetmp"], [rname], out=x1, in0=t[0], in1=t[1], op=ALU.subtract)
            dve("tensor_tensor", ["ropetmp"], [rname], out=x2, in0=t[2], in1=t[3], op=ALU.add)

        def rows_post(l, ri, n, st, kpos, qpos, is_new, outd, orow, cidx):
            rw, rname = rows[ri], f"rows{ri}"
            blk = kpos // 128
            sk = st.name
            if is_new:
                act(junk[0:n, 0:256], rw[0:n, CKV[0]:CKV[1]], AF.Square, [rname], ["tV", "small"], accum_out=small[0:n, 0:1])
                rstd_from_ss(small[0:n, 0:1], small[0:n, 1:2], n, 256, ["small"], ["small"])
                dve("scalar_tensor_tensor", [rname, "small", "kvn"], [rname], out=rw[0:n, CKV[0]:CKV[1]], in0=rw[0:n, CKV[0]:CKV[1]],
                    scalar=small[0:n, 1:2], in1=kvn[0:n, :], op0=ALU.mult, op1=ALU.mult)
                rope_inplace(rw, rname, n, QR[0], 4, "ropet")
                rope_inplace(rw, rname, n, KR[0], 1, "ropet")
                dve("tensor_scalar", [rname], [rname], out=rw[0:n, QR[0]:QR[1]], in0=rw[0:n, QR[0]:QR[1]], scalar1=A_SCALE, scalar2=None, op0=ALU.mult)
                dve("tensor_tensor", [rname, "fbb"], [rname], out=rw[0:n, BF[0]:BF[1]], in0=rw[0:n, BF[0]:BF[1]], in1=fbb[0:n, :], op=ALU.add)
                act(rw[0:n, BF[0]:BF[1]], rw[0:n, BF[0]:BF[1]], AF.Exp, [rname], [rname], scale=-1.0)
                act(rw[0:n, BF[0]:BF[1]], rw[0:n, BF[0]:BF[1]], AF.Ln, [rname], [rname], bias=1.0)
                dve("tensor_scalar", [rname], [rname], out=rw[0:n, BF[0]:BF[1]], in0=rw[0:n, BF[0]:BF[1]], scalar1=-1.0, scalar2=None, op0=ALU.mult)
                for nm, cr in (("ckv", CKV), ("kpe", KR), ("fk", BK), ("fv", BV), ("lf", BF), ("sk", CK), ("sv", CV)):
                    dma(outd[nm][l, orow:orow + n, :], rw[0:n, cr[0]:cr[1]], [rname], [], f"o_{rname}", eng="sp")
            pa, pb, pc = 0, 1, 2
            if is_new:
                tr(ps[pa][:, 0:n], rw[0:n, QR[0]:QR[1]], n, [rname], [PSN[pa]])
            tr(ps[pa][:, 128:128 + n], rw[0:n, CKV[0]:CKV[0] + 128], n, [rname], [PSN[pa]])
            tr(ps[pa][:, 256:256 + n], rw[0:n, CKV[0] + 128:CKV[1]], n, [rname], [PSN[pa]])
            tr(ps[pa][0:32, 384:384 + n], rw[0:n, KR[0]:KR[1]], n, [rname], [PSN[pa]])
            tr(ps[pb][:, 0:n], rw[0:n, BK[0]:BK[0] + 128], n, [rname], [PSN[pb]])
            tr(ps[pb][:, 128:128 + n], rw[0:n, BK[0] + 128:BK[1]], n, [rname], [PSN[pb]])
            tr(ps[pb][:, 256:256 + n], rw[0:n, CK[0]:CK[0] + 128], n, [rname], [PSN[pb]])
            tr(ps[pb][:, 384:384 + n], rw[0:n, CK[0] + 128:CK[1]], n, [rname], [PSN[pb]])
            tr(ps[pc][0:4, 0:n], rw[0:n, BF[0]:BF[1]], n, [rname], [PSN[pc]])
            psa = ps[pa][:, :].rearrange("p (j t) -> p j t", j=4)
            psb = ps[pb][:, :].rearrange("p (j t) -> p j t", j=4)
            j0 = 0 if is_new else 1
            act(tA[:, j0:3, 0:n], psa[:, j0:3, 0:n], AF.Copy, [PSN[pa]], ["tA"])
            act(tA[0:32, 3, 0:n], psa[0:32, 3, 0:n], AF.Copy, [PSN[pa]], ["tA"])
            dve("tensor_copy", [PSN[pb]], ["tB"], out=tB[:, :, 0:n], in_=psb[:, :, 0:n])
            dve("tensor_copy", [PSN[pc]], ["lfT"], out=lfT[0:4, 0:n], in_=ps[pc][0:4, 0:n])
            if is_new:
                for h in range(4):
                    dma(st.Q["A"][h, 64:96, qpos:qpos + n], tA[h * 32:(h + 1) * 32, 0, 0:n], ["tA"], [(sk, "QA")], "st_tA", eng="pool")
            dma(st.KPE[:, kpos:kpos + n], tA[0:32, 3, 0:n], ["tA"], [(sk, "KPE")], "st_tA", eng="pool")
            for h in range(4):
                dma(st.KB[h, 0:64, kpos:kpos + n], tB[(h % 2) * 64:(h % 2) * 64 + 64, h // 2, 0:n], ["tB"], [(sk, "KB")], "st_tB", eng="pool")
                dma(st.KC[h, 0:64, kpos:kpos + n], tB[(h % 2) * 64:(h % 2) * 64 + 64, 2 + h // 2, 0:n], ["tB"], [(sk, "KC")], "st_tB", eng="pool")
            pk, pv = 3, 4
            for cc in range(2):
                for rc in range(2):
                    mm(ps[pk][:, cc * 128:cc * 128 + n], Wuk[:, rc, cc * 128:(cc + 1) * 128], tA[:, 1 + rc, 0:n], rc == 0, rc == 1,
                       ["tA", "Wuk"], [PSN[pk]])
            for rc in range(2):
                mm(ps[pv][0:n, :], tA[:, 1 + rc, 0:n], Wuv[:, rc, :], rc == 0, rc == 1, ["tA", "Wuv"], [PSN[pv]])
            act(tK[:, :, 0:n], ps[pk][:, 0:256].rearrange("p (j t) -> p j t", j=2)[:, :, 0:n], AF.Copy, [PSN[pk]], ["tK"])
            dve("tensor_copy", [PSN[pv]], ["tV"], out=tV[0:n, :], in_=ps[pv][0:n, :])
            for h in range(4):
                dma(st.KA[h, :, kpos:kpos + n], tK[(h % 2) * 64:(h % 2) * 64 + 64, h // 2, 0:n], ["tK"], [(sk, "KA")], "st_tK", eng="pool")
            dma(st.V["A"][0:n, blk, :, :], tV[0:n, :].rearrange("p (h d) -> p h d", h=4), ["tV"], [(sk, "VA")], "st_tV", eng="pool")
            pool("tensor_copy", [rname], ["tVB"], out=tVB[0:n, :], in_=rw[0:n, BV[0]:BV[1]])
            pool("tensor_copy", [rname], ["tVC"], out=tVC[0:n, :], in_=rw[0:n, CV[0]:CV[1]])
            dma(st.V["B"][0:n, blk, :, :], tVB[0:n, :].rearrange("p (h d) -> p h d", h=4), ["tVB"], [(sk, "VB")], "st_tVB", eng="pool")
            dma(st.V["C"][0:n, blk, :, :], tVC[0:n, :].rearrange("p (h d) -> p h d", h=4), ["tVC"], [(sk, "VC")], "st_tVC", eng="pool")
            cr = carry[0:4, cidx:cidx + 1]
            dve("tensor_tensor_scan", ["onesF", "lfT", "carry"], ["Ft"], out=Ft[0:4, 0:n], data0=onesF[0:4, 0:n], data1=lfT[0:4, 0:n],
                initial=cr, op0=ALU.mult, op1=ALU.add)
            dve("tensor_copy", ["Ft"], ["carry"], out=cr, in_=Ft[0:4, n - 1:n])
            dve("tensor_copy", ["Ft"], ["Fq3"], out=Fq3[0:4, 0, 0:n], in_=Ft[0:4, 0:n])
            dve("tensor_tensor", ["Ft", "Fq3"], ["lfT"], out=Fr[0:4, 0:n], in0=Ft[0:4, 0:n], in1=Fq3[0:4, 0, 0:n], op=ALU.subtract)
            dve("tensor_copy", ["lfT"], ["Fq3"], out=Fq3[0:4, 1, 0:n], in_=Fr[0:4, 0:n])
            dve("tensor_tensor", ["lfT", "Fq3"], ["lfT"], out=Fr[0:4, 0:n], in0=Fr[0:4, 0:n], in1=Fq3[0:4, 1, 0:n], op=ALU.subtract)
            dve("tensor_copy", ["lfT"], ["Fq3"], out=Fq3[0:4, 2, 0:n], in_=Fr[0:4, 0:n])
            dve("tensor_scalar", ["Fq3"], ["Fk3"], out=Fk3[0:4, :, 0:n], in0=Fq3[0:4, :, 0:n], scalar1=-1.0, scalar2=None, op0=ALU.mult)
            dma(st.KB[:, 67:70, kpos:kpos + n], Fk3[0:4, :, 0:n], ["Fk3"], [(sk, "KB")], "st_F", eng="pool")
            if is_new:
                dma(st.Q["B"][:, 64:67, qpos:qpos + n], Fq3[0:4, :, 0:n], ["Fq3"], [(sk, "QB")], "st_F", eng="pool")

        def p1_tile(l, x_src, slabs, subs, outd, cos_d, sin_d, xkey):
            TT = sum(s[1] for s in slabs)
            for si, (r0, n, c0) in enumerate(slabs):
                xi = cnt["ld"] % 2
                cnt["ld"] += 1
                xn = f"xsl{xi}"
                dma(xsl[xi][0:n, :], x_src[r0:r0 + n, :], [xkey], [xn], f"ld_{xn}")
                act(hsl[0:n, :], xsl[xi][0:n, :], AF.Square, [xn], ["hsl", "small"], accum_out=small[0:n, 2:3])
                rstd_from_ss(small[0:n, 2:3], small[0:n, 3:4], n, D, ["small"], ["small"])
                dve("tensor_scalar", [xn, "small"], ["hsl"], out=hsl[0:n, :], in0=xsl[xi][0:n, :], scalar1=small[0:n, 3:4], scalar2=None, op0=ALU.mult)
                for half in range(2):
                    pb_ = 5 + half
                    for j in range(4):
                        kc = half * 4 + j
                        tr(ps[pb_][:, j * 128:j * 128 + n], hsl[0:n, kc * 128:(kc + 1) * 128], n, ["hsl"], [PSN[pb_]])
                    for j in range(4):
                        kc = half * 4 + j
                        if j % 2 == 0:
                            act(hT[:, kc, c0:c0 + n], ps[pb_][:, j * 128:j * 128 + n], AF.Copy, [PSN[pb_], "gpre"], ["hT"], scale=gpre[:, kc:kc + 1])
                        else:
                            dve("tensor_scalar", [PSN[pb_], "gpre"], ["hT"], out=hT[:, kc, c0:c0 + n], in0=ps[pb_][:, j * 128:j * 128 + n],
                                scalar1=gpre[:, kc:kc + 1], scalar2=None, op0=ALU.mult)
            fsubs = []
            for sub in subs:
                if fsubs and fsubs[-1][2] is sub[2] and fsubs[-1][0] + fsubs[-1][1] == sub[0] and fsubs[-1][3] + fsubs[-1][1] == sub[3]:
                    p = fsubs[-1]
                    fsubs[-1] = (p[0], p[1] + sub[1], p[2], p[3], p[4], p[5], p[6])
                else:
                    fsubs.append(tuple(sub))
            def fm_chunk(c):
                pb_ = 5 + (c % 3)
                for kc in range(8):
                    mm(ps[pb_][:, 0:TT], Wf[:, kc, c * 128:(c + 1) * 128], hT[:, kc, 0:TT], kc == 0, kc == 7, ["Wf", "hT"], [PSN[pb_]])
                fi = c % 2
                fn = f"fev{fi}"
                if c < 6:
                    g = "ABC"[c // 2]
                    sc = (A_SCALE, B_SCALE, C_SCALE)[c // 2]
                    dve("tensor_scalar", [PSN[pb_]], [fn], out=fev[fi][:, 0:TT], in0=ps[pb_][:, 0:TT], scalar1=sc, scalar2=None, op0=ALU.mult)
                    for (sc0, n, st, qpos, orow, cidx, trow) in fsubs:
                        for hh in range(2):
                            h = (c % 2) * 2 + hh
                            dma(st.Q[g][h, 0:64, qpos:qpos + n], fev[fi][hh * 64:hh * 64 + 64, sc0:sc0 + n], [fn], [(st.name, "Q" + g)], f"st_{fn}", eng="act")
                else:
                    act(fev[fi][:, 0:TT], ps[pb_][:, 0:TT], AF.Silu, [PSN[pb_]], [fn])
                    gc = c - 6
                    for (sc0, n, st, qpos, orow, cidx, trow) in fsubs:
                        if gc < 4:
                            dma(st.G["A"][gc, :, qpos:qpos + n], fev[fi][:, sc0:sc0 + n], [fn], [(st.name, "GA")], f"st_{fn}", eng="act")
                        else:
                            g = "B" if gc < 6 else "C"
                            for hh in range(2):
                                h = (gc % 2) * 2 + hh
                                dma(st.G[g][h, :, qpos:qpos + n], fev[fi][hh * 64:hh * 64 + 64, sc0:sc0 + n], [fn], [(st.name, "G" + g)], f"st_{fn}", eng="act")
            pend = None
            fm_left = list(range(14))
            fm_per = -(-14 // len(subs))

            def post(p):
                (sc0, n, st, qpos, orow, cidx, trow), ri = p
                dma(ropet[0:n, 0, :], cos_d[trow:trow + n, :], [], ["ropet"], "ld_rope")
                dma(ropet[0:n, 1, :], sin_d[trow:trow + n, :], [], ["ropet"], "ld_rope")
                rows_post(l, ri, n, st, st.key_off + qpos, qpos, True, outd, orow, cidx)

            for sub in subs:
                (sc0, n, st, qpos, orow, cidx, trow) = sub
                ri = cnt["ld"] % 2
                cnt["ld"] += 1
                rname = f"rows{ri}"
                groups = ((0, 420), (420, 932), (932, 1444))
                for gi, (g0, g1) in enumerate(groups):
                    pb_ = 5 + gi
                    for kc in range(8):
                        mm(ps[pb_][0:n, 0:g1 - g0], hT[:, kc, sc0:sc0 + n], Wt[:, kc, g0:g1], kc == 0, kc == 7, ["hT", "Wt"], [PSN[pb_]])
                    if gi == 1:
                        dve("tensor_copy", [PSN[pb_]], [rname], out=rows[ri][0:n, g0:g1], in_=ps[pb_][0:n, 0:g1 - g0])
                    else:
                        act(rows[ri][0:n, g0:g1], ps[pb_][0:n, 0:g1 - g0], AF.Copy, [PSN[pb_]], [rname])
                for _ in range(fm_per):
                    if fm_left:
                        fm_chunk(fm_left.pop(0))
                if pend is not None:
                    post(pend)
                pend = (sub, ri)
            post(pend)
            while fm_left:
                fm_chunk(fm_left.pop(0))

        def cache_block(l, b, st, kb):
            ri = cnt["ld"] % 2
            cnt["ld"] += 1
            rname = f"rows{ri}"
            k0 = kb * 128
            for nm, cr in (("ckv", CKV), ("kpe", KR), ("fk", BK), ("fv", BV), ("lf", BF), ("sk", CK), ("sv", CV)):
                dma(rows[ri][:, cr[0]:cr[1]], c_in[nm][l, b, k0:k0 + 128, :], [], [rname], f"ld_{rname}")
            rows_post(l, ri, 128, st, k0, None, False, None, None, 1 + b)

        def attend(l, st, q0, NQ, x_src, x_dst, xrow0, xkey):
            sk = st.name
            nfull_new = NQ // 128
            blocks = []
            if st.key_off == 0:
                lastb = (q0 + NQ) // 128 - 1
                for bi in range(lastb, -1, -1):
                    m = bi - q0 // 128
                    blocks.append((bi, 128, m if m >= 0 else None))
            else:
                blocks.append((st.key_off // 128, NQ, 0))
                for bi in range(st.key_off // 128 - 1, -1, -1):
                    blocks.append((bi, 128, None))
            nb = len(blocks)
            chunks = []
            i = 0
            while i < nb:
                j = min(nb, i + KCH)
                if blocks[i][1] != 128:
                    j = i + 1
                chunks.append((i, j))
                i = j
            chunk_of = {}
            for ci, (c0, c1) in enumerate(chunks):
                for i in range(c0, c1):
                    chunk_of[i] = ci

            def head_stream(g, h, sx):
                dk, dv = DK[g], DV[g]
                gi = "ABC".index(g)
                SB = (0, 1) if sx == 0 else (4, 5)
                AUX = 2 if sx == 0 else 6
                ACC = 3 if sx == 0 else 7
                qi = 2 * sx + cnt["q%d" % sx] % 2
                cnt["q%d" % sx] += 1
                qn = f"qt{qi}"
                dma(qt[qi][0:dk, 0:NQ], st.Q[g][h, :, q0:q0 + NQ], [(sk, "Q" + g), (sk, "QBones")], [qn], f"ld_{qn}")
                kinfo = [None] * nb
                loaded = set()

                def load_chunk(ci):
                    c0, c1 = chunks[ci]
                    slot = 2 * sx + cnt["k%d" % sx] % 2
                    cnt["k%d" % sx] += 1
                    kn, vn = f"kt{slot}", f"vt{slot}"
                    blo = blocks[c1 - 1][0]
                    nblk = c1 - c0
                    klo = blo * 128
                    nkeys = sum(blocks[i][1] for i in range(c0, c1))
                    if g == "A":
                        dma(kt[slot][0:64, 0:nkeys], st.KA[h, :, klo:klo + nkeys], [(sk, "KA")], [kn], f"ld_{kn}")
                        dma(kt[slot][64:96, 0:nkeys], st.KPE[:, klo:klo + nkeys], [(sk, "KPE")], [kn], f"ld_{kn}")
                    elif g == "B":
                        dma(kt[slot][0:70, 0:nkeys], st.KB[h, :, klo:klo + nkeys], [(sk, "KB"), (sk, "KBones")], [kn], f"ld_{kn}")
                    else:
                        dma(kt[slot][0:96, 0:nkeys], st.KC[h, :, klo:klo + nkeys], [(sk, "KC"), (sk, "KBones")], [kn], f"ld_{kn}")
                    npart = 128 if blocks[c0][1] == 128 else blocks[c0][1]
                    dma(vt[slot][0:npart, 0:nblk, 0:dv], st.V[g][0:npart, blo:blo + nblk, h, :], [(sk, "V" + g)], [vn], f"ld_{vn}")
                    for i in range(c0, c1):
                        bi, nk, m = blocks[i]
                        off = (bi - blo) * 128
                        kinfo[i] = (kt[slot][0:dk, off:off + nk], vt[slot][0:nk, bi - blo, :], kn, vn)

                def need(i):
                    ci = chunk_of[i]
                    if ci not in loaded:
                        loaded.add(ci)
                        load_chunk(ci)

                def cst(i):
                    m = blocks[i][2]
                    return 128 * m if (m is not None and st.key_off == 0) else 0

                def qk(i, bank):
                    bi, nk, m = blocks[i]
                    kap, vap, kn, vn = kinfo[i]
                    c0 = cst(i)
                    mm(ps[bank][0:nk, c0:NQ], kap, qt[qi][0:dk, c0:NQ], True, m is None, [kn, qn], [PSN[bank]])
                    if m is not None:
                        mm(ps[bank][0:nk, c0:NQ], ident_b[0:nk, 0:nk], masks[0:nk, gi * 4 + m, c0:NQ], False, True,
                           ["ident_b", "masks"], [PSN[bank]])

                if g in "AB":
                    DEN = AUX
                    need(0)
                    qk(0, SB[0])
                    for i in range(nb + 1):
                        if i >= 1:
                            j = i - 1
                            bi, nk, m = blocks[j]
                            kap, vap, kn, vn = kinfo[j]
                            pj = 2 * sx + j % 2
                            c0 = cst(j)
                            mm(ps[ACC][:, c0:NQ], vap, pt[pj][0:nk, c0:NQ], j == 0, j == nb - 1, [vn, f"pt{pj}"], [PSN[ACC]], skip=True)
                            mm(ps[DEN][:, c0:NQ], ones_b[0:nk, 0:128], pt[pj][0:nk, c0:NQ], j == 0, j == nb - 1, ["ones_b", f"pt{pj}"], [PSN[DEN]], skip=True)
                        yield
                        if i < nb:
                            bi, nk, m = blocks[i]
                            bank = SB[i % 2]
                            pi = 2 * sx + i % 2
                            c0 = cst(i)
                            act(pt[pi][0:nk, c0:NQ], ps[bank][0:nk, c0:NQ], AF.Exp, [PSN[bank]], [f"pt{pi}"])
                        yield
                        if i + 1 < nb:
                            need(i + 1)
                            qk(i + 1, SB[(i + 1) % 2])
                        yield
                        yield
                    dve("reciprocal", [PSN[DEN]], ["sqt"], out=rbt[0:dv, 0:NQ], in_=ps[DEN][0:dv, 0:NQ])
                    dve("tensor_tensor", [PSN[ACC], "sqt"], [("OC" if g == "C" else "Oall")], out=Og[g][0:dv, h, 0:NQ], in0=ps[ACC][0:dv, 0:NQ], in1=rbt[0:dv, 0:NQ], op=ALU.mult)
                else:
                    RB = AUX
                    def act1(t):
                        bi, nk, m = blocks[t]
                        zb, s3 = SB[t % 2], t % 3
                        c0 = cst(t)
                        act(et[s3][0:nk, c0:NQ], ps[zb][0:nk, c0:NQ], AF.Exp, [PSN[zb]], [f"et{s3}"])
                        act(spt[s3][0:nk, c0:NQ], et[s3][0:nk, c0:NQ], AF.Ln, [f"et{s3}"], [f"spt{s3}"], bias=1.0)

                    def pv(t):
                        bi, nk, m = blocks[t]
                        kap, vap, kn, vn = kinfo[t]
                        s2 = t % 2
                        c0 = cst(t)
                        mm(ps[ACC][:, c0:NQ], vap, wt_[s2][0:nk, c0:NQ], t == 0, t == nb - 1, [vn, f"wt{s2}"], [PSN[ACC]], skip=True)

                    need(0)
                    qk(0, SB[0])
                    if nb > 1:
                        need(1)
                        qk(1, SB[1])
                    act1(0)
                    for t in range(nb + 1):
                        if t < nb:
                            bi, nk, m = blocks[t]
                            s2, s3 = t % 2, t % 3
                            c0 = cst(t)
                            mm(ps[RB][:, c0:NQ], trin_b[0:nk, 0, :], spt[s3][0:nk, c0:NQ], t == 0, False, ["trin_b", f"spt{s3}"], [PSN[RB]], skip=True)
                        if t >= 1:
                            pv(t - 1)
                        yield
                        if t < nb:
                            act(argt[s2][0:nk, c0:NQ], ps[RB][0:nk, c0:NQ], AF.Exp, [PSN[RB]], [f"argt{s2}"], scale=-1.0)
                            if t + 1 < nb:
                                act1(t + 1)
                        yield
                        if t + 2 < nb:
                            need(t + 2)
                            qk(t + 2, SB[t % 2])
                        yield
                        if t < nb:
                            mm(ps[RB][:, c0:NQ], trin_b[0:nk, 1, :], spt[s3][0:nk, c0:NQ], False, t == nb - 1, ["trin_b", f"spt{s3}"], [PSN[RB]], skip=True)
                            dve("tensor_tensor", [f"et{s3}", f"argt{s2}"], [f"wt{s2}"], out=wt_[s2][0:nk, c0:NQ], in0=et[s3][0:nk, c0:NQ],
                                in1=argt[s2][0:nk, c0:NQ], op=ALU.mult)
                        yield
                    act(Og[g][0:dv, h, 0:NQ], ps[ACC][0:dv, 0:NQ], AF.Copy, [PSN[ACC]], [("OC" if g == "C" else "Oall")])

            def run_pair(gens):
                alive = list(gens)
                while alive:
                    for gen in list(alive):
                        try:
                            next(gen)
                        except StopIteration:
                            alive.remove(gen)

            def merge(g):
                dv = DV[g]
                dma(gtile[g][0:dv, :, 0:NQ], st.G[g][:, :, q0:q0 + NQ].rearrange("h p t -> p h t"), [(sk, "G" + g)], ["gtall"], "ld_gt")
                width = 4 * dv
                for h in range(4):
                    act(sqt[0:dv, 0:NQ], Og[g][0:dv, h, 0:NQ], AF.Square, [("OC" if g == "C" else "Oall")], ["sqt"])
                    mm(ps[2][:, 0:NQ], ones_f[0:dv, 0:128], sqt[0:dv, 0:NQ], h == 0, h == 3, ["ones_f", "sqt"], [PSN[2]])
                act(lnt[:, 0:NQ], ps[2][:, 0:NQ], AF.Ln, [PSN[2]], ["lnt"], scale=1.0 / width, bias=EPS)
                act(rst[:, 0:NQ], lnt[:, 0:NQ], AF.Exp, ["lnt"], ["lnt"], scale=-0.5)
                for h in range(4):
                    dve("scalar_tensor_tensor", [("OC" if g == "C" else "Oall"), "gout", "lnt"], ["sqt"], out=tmpm[0:dv, 0:NQ], in0=Og[g][0:dv, h, 0:NQ],
                        scalar=gout[g][0:dv, h:h + 1], in1=rst[0:dv, 0:NQ], op0=ALU.mult, op1=ALU.mult)
                    dve("tensor_tensor", ["sqt", "gtall"], ["M" + g], out=Mg[g][0:dv, h, 0:NQ], in0=tmpm[0:dv, 0:NQ], in1=gtile[g][0:dv, h, 0:NQ], op=ALU.mult)

            for h in range(4):
                run_pair([head_stream("A", h, 0), head_stream("C", h, 1)])
            merge("A")
            merge("C")
            run_pair([head_stream("B", 0, 0), head_stream("B", 1, 1)])
            run_pair([head_stream("B", 2, 0), head_stream("B", 3, 1)])
            merge("B")
            for s0 in range(0, NQ, 128):
                n = min(128, NQ - s0)
                dma(xres[0:n, :], x_src[xrow0 + s0:xrow0 + s0 + n, :], [xkey], ["xsl0"], "ld_xres")
                for half in range(2):
                    bank = 4 + half
                    k = 0
                    for g in "ABC":
                        dv = DV[g]
                        Wo = {"A": WoA, "B": WoB, "C": WoC}[g]
                        for h in range(4):
                            mm(ps[bank][0:n, :], Mg[g][:, h, s0:s0 + n], Wo[:, h, half * 512:(half + 1) * 512], k == 0, k == 11,
                               ["M" + g, "Wo" + g], [PSN[bank]])
                            k += 1
                    act(junk[0:n, 0:512], ps[bank][0:n, :], AF.Square, [PSN[bank]], ["tV", "small"], accum_out=small[0:n, 4 + half:5 + half])
                dve("tensor_tensor", ["small"], ["small"], out=small[0:n, 6:7], in0=small[0:n, 4:5], in1=small[0:n, 5:6], op=ALU.add)
                rstd_from_ss(small[0:n, 6:7], small[0:n, 7:8], n, D, ["small"], ["small"])
                for half in range(2):
                    bank = 4 + half
                    cs_ = slice(half * 512, (half + 1) * 512)
                    dve("scalar_tensor_tensor", [PSN[bank], "small", "gpost"], ["xsl1"], out=xnew[0:n, cs_], in0=ps[bank][0:n, :],
                        scalar=small[0:n, 7:8], in1=gpost[0:n, cs_], op0=ALU.mult, op1=ALU.mult)
                dve("tensor_tensor", ["xsl1", "xsl0"], ["xsl1"], out=xnew[0:n, :], in0=xnew[0:n, :], in1=xres[0:n, :], op=ALU.add)
                dma(x_dst[xrow0 + s0:xrow0 + s0 + n, :], xnew[0:n, :], ["xsl1"], [xkey], "st_xnew", eng="pool")

        for l in range(DEPTH):
            load_layer(l)
            pool("memset", [], ["carry"], carry[:, :], 0.0)
            xsrc_p = xp if l == 0 else xbuf_p
            xdst_p = yp if l == DEPTH - 1 else xbuf_p
            xsrc_s = xs if l == 0 else xbuf_s
            xdst_s = ys if l == DEPTH - 1 else xbuf_s
            for t0 in range(0, SEQ, 512):
                TT = min(512, SEQ - t0)
                slabs = [(t0 + j * 128, 128, j * 128) for j in range(TT // 128)]
                subs = [(j * 128, 128, stp, t0 + j * 128, t0 + j * 128, 0, t0 + j * 128) for j in range(TT // 128)]
                p1_tile(l, xsrc_p, slabs, subs, outp, cosp_d, sinp_d, "xbufP")
            S.barrier("sp")
            slabs = [(0, TS, 0)]
            subs = [(b * DS, DS, sts[b], 0, b * DS, 1 + b, 0) for b in range(NSB)]
            tasks = [(lambda b=b, kb=kb: cache_block(l, b, sts[b], kb)) for b in range(NSB) for kb in range(PAST // 128)]
            tasks.append(lambda: p1_tile(l, xsrc_s, slabs, subs, outs, coss_d, sins_d, "xbufS"))
            nqt = SEQ // NQP
            per = -(-len(tasks) // nqt)
            ti = 0
            for q0 in range(0, SEQ, NQP):
                attend(l, stp, q0, NQP, xsrc_p, xdst_p, q0, "xbufP")
                for _ in range(per):
                    if ti < len(tasks):
                        tasks[ti]()
                        ti += 1
            while ti < len(tasks):
                tasks[ti]()
                ti += 1
            S.barrier("sp")
            for b in range(NSB):
                attend(l, sts[b], 0, DS, xsrc_s, xdst_s, b * DS, "xbufS")
            S.barrier("sp")

        S.build()
        sems = {}
        for e in S.ENGS:
            sems[("E", e)] = es.enter_context(nc.semaphore("s_" + e))
        for i, k in enumerate(S.dma_keys):
            sems[("D", k)] = es.enter_context(nc.semaphore(f"d{i}"))
        with nc.Block() as block:
            @block.tensor
            def _(eng):
                S.emit("pe", eng, sems)

            @block.scalar
            def _(eng):
                S.emit("act", eng, sems)

            @block.vector
            def _(eng):
                S.emit("dve", eng, sems)

            @block.gpsimd
            def _(eng):
                S.emit("pool", eng, sems)

            @block.sync
            def _(eng):
                S.emit("sp", eng, sems)
    nc._sched_stats = {e: len(S.ops[e]) for e in S.ENGS}
    return nc


def make_consts(SEQ, PAST):
    idx = np.arange(128)
    ident = np.eye(128, dtype=np.float32)
    trin = np.stack([(idx[:, None] >= idx[None, :]), (idx[:, None] < idx[None, :])]).astype(np.float32)
    kl = np.arange(128)[:, None]
    ql = np.arange(512)[None, :]
    masks = np.zeros((12, 128, 512), np.float32)
    for m in range(4):
        kp = 128 * m + kl
        masks[0 + m] = np.where((kp // 64) <= (ql // 64), 0.0, NEG)
        masks[4 + m] = np.where(kp <= ql, 0.0, NEG)
        masks[8 + m] = np.where(kp < ql, 0.0, NEG)
    half = 16
    inv = (10000.0 ** (-np.arange(half, dtype=np.float32) / half)).astype(np.float32)

    def tables(pos):
        ang = pos.astype(np.float32)[:, None] * inv[None, :]
        c = np.cos(ang).astype(np.float32)
        s = np.sin(ang).astype(np.float32)
        return np.tile(c, (1, 4)), np.tile(s, (1, 4))

    cosp, sinp = tables(np.arange(SEQ))
    coss, sins = tables(PAST + np.arange(DS))
    return dict(ident=ident, trin=trin, masks=masks, cosp=cosp, sinp=sinp, coss=coss, sins=sins)


def split_w_in(w_in):
    sizes = (256, 128, 256, 32, 256, 256, 256, 4, 256, 256, 256, 1024)
    offs = np.cumsum((0,) + sizes)
    seg = lambda i: w_in[:, :, offs[i]:offs[i + 1]]
    a_qn, a_qr, a_ckv, a_kr, b_q, b_k, b_v, b_f, c_q, c_k, c_v, gate = (seg(i) for i in range(12))
    wf = np.concatenate([a_qn, b_q, c_q, gate], axis=-1)
    wt = np.concatenate([a_qr, a_ckv, a_kr, b_f, b_k, b_v, c_k, c_v], axis=-1)
    return np.ascontiguousarray(wf), np.ascontiguousarray(wt)


_NC_CACHE = {}


def run(inputs, SEQ, PAST, DEPTH, n_cores, NSB):
    f = lambda a: np.ascontiguousarray(np.asarray(a, dtype=np.float32))
    key = (SEQ, PAST, DEPTH, NSB)
    if key not in _NC_CACHE:
        _NC_CACHE[key] = build(SEQ, PAST, DEPTH, NSB)
    nc = _NC_CACHE[key]
    consts = make_consts(SEQ, PAST)
    wf, wt = split_w_in(f(inputs["w_in"]))
    common = dict(wf=wf, wt=wt, wuk=f(inputs["mla_w_uk"]).reshape(DEPTH, 256, 256), wuv=f(inputs["mla_w_uv"]).reshape(DEPTH, 256, 512),
                  wo=f(inputs["w_out"]), npre=f(inputs["norm_pre"]), npost=f(inputs["norm_post"]), kvn=f(inputs["mla_kv_norm"]),
                  fb=f(inputs["fox_forget_bias"]), onorm=f(inputs["out_norm"]), **consts)
    xp, xs = f(inputs["x_prompt"]), f(inputs["x_sample"])
    nseq = xp.shape[0]
    cpp = n_cores // nseq
    in_maps = []
    for c in range(n_cores):
        b0 = c * NSB
        m = dict(common)
        m["xp"] = xp[c // cpp]
        m["xs"] = np.ascontiguousarray(xs[b0:b0 + NSB].reshape(NSB * DS, D))
        m["c_ckv"] = f(inputs["cache_mla_ckv"][:, b0:b0 + NSB])
        m["c_kpe"] = f(inputs["cache_mla_kpe"][:, b0:b0 + NSB])
        m["c_fk"] = f(inputs["cache_fox_k"][:, b0:b0 + NSB]).reshape(DEPTH, NSB, PAST, 256)
        m["c_fv"] = f(inputs["cache_fox_v"][:, b0:b0 + NSB]).reshape(DEPTH, NSB, PAST, 256)
        m["c_lf"] = f(inputs["cache_fox_logf"][:, b0:b0 + NSB])
        m["c_sk"] = f(inputs["cache_sb_k"][:, b0:b0 + NSB]).reshape(DEPTH, NSB, PAST, 256)
        m["c_sv"] = f(inputs["cache_sb_v"][:, b0:b0 + NSB]).reshape(DEPTH, NSB, PAST, 256)
        in_maps.append(m)
    res = run_bass_kernel_spmd(nc, in_maps, core_ids=list(range(n_cores)))
    R = res.results
    pc = [R[i * cpp] for i in range(nseq)]
    y_p = np.stack([r["yp"] for r in pc], 0)
    y_s = np.concatenate([r["ys"].reshape(NSB, DS, D) for r in R], 0)
    outs = [y_p, y_s]
    shp = {"ckv": (256,), "kpe": (32,), "fk": (4, 64), "fv": (4, 64), "lf": (4,), "sk": (4, 64), "sv": (4, 64)}
    for nm in ("ckv", "kpe", "fk", "fv", "lf", "sk", "sv"):
        a = np.stack([r["p_" + nm] for r in pc], 1)
        outs.append(a.reshape(a.shape[:3] + shp[nm]))
    for nm in ("ckv", "kpe", "fk", "fv", "lf", "sk", "sv"):
        a = np.concatenate([r["s_" + nm].reshape(DEPTH, NSB, DS, -1) for r in R], 1)
        outs.append(a.reshape(a.shape[:3] + shp[nm]))
    return tuple(np.ascontiguousarray(o, dtype=np.float32) for o in outs)


def kernel(**inputs):
    return run(inputs, SEQ=8192, PAST=2048, DEPTH=4, n_cores=8, NSB=4)
```

```python
import math
from contextlib import ExitStack

import numpy as np
import concourse.bass as bass
import concourse.mybir as mybir
from concourse.bass_utils import run_bass_kernel_spmd

F32 = mybir.dt.float32
BF16 = mybir.dt.bfloat16
AF = mybir.ActivationFunctionType
ALU = mybir.AluOpType

D = 1024
A_SCALE = 96.0 ** -0.5
B_SCALE = 64.0 ** -0.5
C_SCALE = 64.0 ** -0.5
EPS = 1e-6
NEG = -30000.0
DS = 16
QR, CKV, KR, BF, BK, BV, CK, CV = (0, 128), (128, 384), (384, 416), (416, 420), (420, 676), (676, 932), (932, 1188), (1188, 1444)
NT_COLS = 1444
NF_COLS = 1792
DK = {"A": 96, "B": 70, "C": 96}
DV = {"A": 128, "B": 64, "C": 64}


class Op:
    __slots__ = ("eng", "fn", "deps", "signal", "dsem", "tok", "idx")

    def __init__(self, eng, fn, dsem):
        self.eng = eng
        self.fn = fn
        self.deps = []
        self.signal = False
        self.dsem = dsem
        self.tok = None
        self.idx = 0


class Sched:
    ENGS = ("pe", "act", "dve", "pool", "sp")

    def __init__(self):
        self.ops = {e: [] for e in self.ENGS}
        self.res = {}
        self.dma_keys = []
        self.n = 0
        self.final_dma = {}

    def add(self, eng, fn, reads=(), writes=(), dsem=None):
        op = Op(eng, fn, dsem)
        op.idx = self.n
        self.n += 1
        if dsem is not None and dsem not in self.dma_keys:
            self.dma_keys.append(dsem)
        deps = {}
        for r in reads:
            st = self.res.get(r)
            if st is None:
                st = self.res[r] = [None, []]
            if st[0] is not None:
                deps[id(st[0])] = st[0]
            st[1].append(op)
        for w in writes:
            st = self.res.get(w)
            if st is None:
                st = self.res[w] = [None, []]
            if st[0] is not None:
                deps[id(st[0])] = st[0]
            for rd in st[1]:
                if rd is not op:
                    deps[id(rd)] = rd
            st[0] = op
            st[1] = []
        for d in deps.values():
            if d is op:
                continue
            if d.dsem is None and op.dsem is None and d.eng == "pe" and op.eng == "pe":
                continue
            if d.dsem is not None and op.dsem is not None and d.dsem == op.dsem and d.eng == op.eng:
                continue
            d.signal = True
            op.deps.append(d)
        self.ops[eng].append(op)
        return op

    def barrier(self, eng="sp"):
        op = Op(eng, None, None)
        op.idx = self.n
        self.n += 1
        last = {}
        for e in self.ENGS:
            for o in self.ops[e]:
                if o.dsem is not None and (o.dsem not in last or o.idx > last[o.dsem].idx):
                    last[o.dsem] = o
        op.deps = list(last.values())
        self.ops[eng].append(op)

    def build(self):
        for e in self.ENGS:
            c = 0
            for op in self.ops[e]:
                if op.dsem is None and op.signal:
                    c += 1
                    op.tok = (("E", e), c)
        dcount = {}
        order = sorted((op for e in self.ENGS for op in self.ops[e] if op.dsem is not None), key=lambda o: o.idx)
        for op in order:
            dcount[op.dsem] = dcount.get(op.dsem, 0) + 16
            op.tok = (("D", op.dsem), dcount[op.dsem])
        self.final_dma = dict(dcount)

    def emit(self, e, eng, sems):
        known = {}
        for op in self.ops[e]:
            need = {}
            for d in op.deps:
                k, v = d.tok
                if v > need.get(k, 0):
                    need[k] = v
            for k, v in need.items():
                if known.get(k, 0) >= v:
                    continue
                eng.wait_ge(sems[k], v)
                known[k] = v
            if op.fn is None:
                continue
            ins = op.fn(eng)
            if op.dsem is not None:
                ins.then_inc(sems[("D", op.dsem)], 16)
            elif op.signal:
                ins.then_inc(sems[("E", e)], 1)
        if e == "sp":
            for k, v in self.final_dma.items():
                eng.wait_ge(sems[("D", k)], v)


class Stream:
    pass


def build(SEQ, PAST, DEPTH, NSB, NQP=512):
    nc = bass.Bass("TRN2", target_bir_lowering=False)
    S = Sched()
    NKS = PAST + DS
    NBP = SEQ // 128
    NBS = PAST // 128 + 1
    TS = NSB * DS

    def din(name, shape):
        return nc.dram_tensor(name, list(shape), F32, kind="ExternalInput").ap()

    def dout(name, shape):
        return nc.dram_tensor(name, list(shape), F32, kind="ExternalOutput").ap()

    def dint(name, shape, dt=BF16):
        return nc.dram_tensor(name, list(shape), dt, kind="Internal").ap()

    xp = din("xp", [SEQ, D])
    xs = din("xs", [TS, D])
    c_in = {
        "ckv": din("c_ckv", [DEPTH, NSB, PAST, 256]), "kpe": din("c_kpe", [DEPTH, NSB, PAST, 32]),
        "fk": din("c_fk", [DEPTH, NSB, PAST, 256]), "fv": din("c_fv", [DEPTH, NSB, PAST, 256]),
        "lf": din("c_lf", [DEPTH, NSB, PAST, 4]), "sk": din("c_sk", [DEPTH, NSB, PAST, 256]),
        "sv": din("c_sv", [DEPTH, NSB, PAST, 256]),
    }
    wf_d = din("wf", [DEPTH, D, NF_COLS])
    wt_d = din("wt", [DEPTH, D, NT_COLS])
    wuk_d = din("wuk", [DEPTH, 256, 256])
    wuv_d = din("wuv", [DEPTH, 256, 512])
    wo_d = din("wo", [DEPTH, D, D])
    npre_d = din("npre", [DEPTH, D])
    npost_d = din("npost", [DEPTH, D])
    kvn_d = din("kvn", [DEPTH, 256])
    fb_d = din("fb", [DEPTH, 4])
    onorm_d = din("onorm", [DEPTH, D])
    ident_d = din("ident", [128, 128])
    trin_d = din("trin", [2, 128, 128])
    masks_d = din("masks", [12, 128, 512])
    cosp_d = din("cosp", [SEQ, 64])
    sinp_d = din("sinp", [SEQ, 64])
    coss_d = din("coss", [DS, 64])
    sins_d = din("sins", [DS, 64])

    yp = dout("yp", [SEQ, D])
    ys = dout("ys", [TS, D])
    outp = {"ckv": dout("p_ckv", [DEPTH, SEQ, 256]), "kpe": dout("p_kpe", [DEPTH, SEQ, 32]),
            "fk": dout("p_fk", [DEPTH, SEQ, 256]), "fv": dout("p_fv", [DEPTH, SEQ, 256]),
            "lf": dout("p_lf", [DEPTH, SEQ, 4]), "sk": dout("p_sk", [DEPTH, SEQ, 256]),
            "sv": dout("p_sv", [DEPTH, SEQ, 256])}
    outs = {"ckv": dout("s_ckv", [DEPTH, TS, 256]), "kpe": dout("s_kpe", [DEPTH, TS, 32]),
            "fk": dout("s_fk", [DEPTH, TS, 256]), "fv": dout("s_fv", [DEPTH, TS, 256]),
            "lf": dout("s_lf", [DEPTH, TS, 4]), "sk": dout("s_sk", [DEPTH, TS, 256]),
            "sv": dout("s_sv", [DEPTH, TS, 256])}
    xbuf_p = dint("xbuf_p", [SEQ, D], F32)
    xbuf_s = dint("xbuf_s", [TS, D], F32)

    def mkstream(name, T, NK, NB, key_off):
        st = Stream()
        st.name, st.T, st.NK, st.NB, st.key_off = name, T, NK, NB, key_off
        st.Q = {"A": dint(name + "_QA", [4, 96, T]), "B": dint(name + "_QB", [4, 70, T]), "C": dint(name + "_QC", [4, 96, T])}
        st.G = {"A": dint(name + "_GA", [4, 128, T]), "B": dint(name + "_GB", [4, 64, T]), "C": dint(name + "_GC", [4, 64, T])}
        st.KA = dint(name + "_KA", [4, 64, NK])
        st.KPE = dint(name + "_KPE", [32, NK])
        st.KB = dint(name + "_KB", [4, 70, NK])
        st.KC = dint(name + "_KC", [4, 96, NK])
        st.V = {"A": dint(name + "_VA", [128, NB, 4, 128]), "B": dint(name + "_VB", [128, NB, 4, 64]),
                "C": dint(name + "_VC", [128, NB, 4, 64])}
        return st

    stp = mkstream("P", SEQ, SEQ, NBP, 0)
    sts = [mkstream(f"S{b}", DS, NKS, NBS, PAST) for b in range(NSB)]

    es = ExitStack()
    with es:
        def sb(name, shape, dt=F32):
            return es.enter_context(nc.sbuf_tensor("sb_" + name, list(shape), dt))

        ps = [es.enter_context(nc.psum_tensor(f"ps{i}", [128, 512], F32)) for i in range(8)]
        PSN = [f"ps{i}" for i in range(8)]

        ident = sb("ident", [128, 128])
        ones_f = sb("ones_f", [128, 128])
        ones_b = sb("ones_b", [128, 128], BF16)
        trin_b = sb("trin_b", [128, 2, 128], BF16)
        ident_b = sb("ident_b", [128, 128], BF16)
        masks = sb("masks", [128, 12, 512], BF16)
        wstage = None
        Wf = sb("Wf", [128, 8, NF_COLS], BF16)
        Wt = sb("Wt", [128, 8, NT_COLS], BF16)
        Wuk = sb("Wuk", [128, 2, 256], BF16)
        Wuv = sb("Wuv", [128, 2, 512], BF16)
        WoA = sb("WoA", [128, 4, D], BF16)
        WoB = sb("WoB", [128, 4, D], BF16)
        WoC = sb("WoC", [128, 4, D], BF16)
        gpre = sb("gpre", [128, 8])
        gpost = sb("gpost", [128, D])
        kvn = sb("kvn", [128, 256])
        fbb = sb("fbb", [128, 4])
        gout = {"A": sb("goutA", [128, 4]), "B": sb("goutB", [64, 4]), "C": sb("goutC", [64, 4])}
        xsl = [sb(f"xsl{i}", [128, D]) for i in range(2)]
        hsl = sb("hsl", [128, D])
        junk = None
        small = sb("small", [128, 16])
        hT = sb("hT", [128, 8, 512], BF16)
        fev = [sb(f"fev{i}", [128, 512], BF16) for i in range(2)]
        rows = [sb(f"rows{i}", [128, NT_COLS]) for i in range(2)]
        ropet = sb("ropet", [128, 4, 64])
        ropetmp = sb("ropetmp", [128, 4, 64])
        tA = sb("tA", [128, 4, 128], BF16)
        tB = sb("tB", [128, 4, 128], BF16)
        tK = sb("tK", [128, 2, 128], BF16)
        tV = sb("tV", [128, 512], BF16)
        junk = tV
        tVB = sb("tVB", [128, 256], BF16)
        tVC = sb("tVC", [128, 256], BF16)
        lfT = sb("lfT", [4, 128])
        Ft = sb("Ft", [4, 128])
        Fr = lfT
        Fq3 = sb("Fq3", [4, 3, 128], BF16)
        Fk3 = sb("Fk3", [4, 3, 128], BF16)
        onesF = sb("onesF", [4, 128])
        carry = sb("carry", [4, 1 + NSB])
        ones3 = sb("ones3", [4, 3, 128], BF16)
        qt = [sb(f"qt{i}", [96, NQP], BF16) for i in range(4)]
        KCH = 4
        kt = [sb(f"kt{i}", [96, KCH * 128], BF16) for i in range(4)]
        vt = [sb(f"vt{i}", [128, KCH, 128], BF16) for i in range(4)]
        pt = [sb(f"pt{i}", [128, NQP], BF16) for i in range(4)]
        et = [sb(f"et{i}", [128, NQP]) for i in range(3)]
        wstage = et
        spt = [sb(f"spt{i}", [128, NQP], BF16) for i in range(3)]
        argt = [sb(f"argt{i}", [128, NQP]) for i in range(2)]
        wt_ = [sb(f"wt{i}", [128, NQP], BF16) for i in range(2)]
        rden = None
        rbt = None
        Oall = sb("Oall", [128, 4, NQP])
        OCt = sb("OCt", [128, 4, NQP])
        Og = {"A": Oall, "B": Oall, "C": OCt}
        sqt = sb("sqt", [128, NQP])
        lnt = sb("lnt", [128, NQP])
        rden = lnt
        rbt = sqt
        rst = lnt
        tmpm = sqt
        gtall = sb("gtall", [128, 4, NQP], BF16)
        gtile = {"A": gtall, "B": gtall, "C": gtall}
        Mg = {"A": sb("MA", [128, 4, NQP], BF16), "B": sb("MB", [128, 4, NQP], BF16), "C": sb("MC", [128, 4, NQP], BF16)}
        xres, xnew = xsl[0], xsl[1]

        cnt = {"ld": 0, "hd": 0, "q0": 0, "q1": 0, "k0": 0, "k1": 0}

        def dma(out, in_, reads, writes, key, eng="sp", slow=False):
            if slow:
                S.add(eng, lambda e: e.dma_start(out=out, in_=in_, allow_slow_non_contiguous=True), reads, writes, dsem=key)
            else:
                S.add(eng, lambda e: e.dma_start(out=out, in_=in_), reads, writes, dsem=key)

        def mm(out, lhsT, rhs, start, stop, reads, writes, skip=False):
            if skip:
                S.add("pe", lambda e: e.matmul(out, lhsT=lhsT, rhs=rhs, start=start, stop=stop, skip_group_check=True), reads, writes)
            else:
                S.add("pe", lambda e: e.matmul(out, lhsT=lhsT, rhs=rhs, start=start, stop=stop), reads, writes)

        def tr(out, in_, n, reads, writes):
            S.add("pe", lambda e: e.transpose(out=out, in_=in_, identity=ident[0:n, 0:n]), list(reads) + ["ident"], writes)

        def act(out, in_, func, reads, writes, **kw):
            S.add("act", lambda e: e.activation(out=out, in_=in_, func=func, **kw), reads, writes)

        def dve(name, reads, writes, *a, **kw):
            S.add("dve", lambda e: getattr(e, name)(*a, **kw), reads, writes)

        def pool(name, reads, writes, *a, **kw):
            S.add("pool", lambda e: getattr(e, name)(*a, **kw), reads, writes)

        dma(ident[:, :], ident_d, [], ["ident"], "ld_ident")
        pool("memset", [], ["ones_f"], ones_f[:, :], 1.0)
        pool("memset", [], ["ones_b"], ones_b[:, :], 1.0)
        pool("memset", [], ["onesF"], onesF[:, :], 1.0)
        pool("memset", [], ["ones3"], ones3[:, :, :], 1.0)
        for j in range(2):
            dma(wstage[j][:, 0:128], trin_d[j], [], [f"et{j}"], f"ld_ws{j}")
            pool("tensor_copy", [f"et{j}"], ["trin_b"], out=trin_b[:, j, :], in_=wstage[j][:, 0:128])
        pool("tensor_copy", ["ident"], ["ident_b"], out=ident_b[:, :], in_=ident[:, :])
        for i in range(12):
            s = i % 2
            dma(wstage[s][:, 0:512], masks_d[i], [], [f"et{s}"], f"ld_ws{s}")
            pool("tensor_copy", [f"et{s}"], ["masks"], out=masks[:, i, :], in_=wstage[s][:, 0:512])
        pool("memset", [], ["fev0"], fev[0][:, :], 0.0)
        for i in range(4):
            pool("memset", [], [f"vt{i}"], vt[i][:, :, :], 0.0)
        pool("memset", [], ["MB"], Mg["B"][:, :, :], 0.0)
        pool("memset", [], ["MC"], Mg["C"][:, :, :], 0.0)
        pool("memset", [], ["WoB"], WoB[:, :, :], 0.0)
        pool("memset", [], ["WoC"], WoC[:, :, :], 0.0)
        for st in [stp] + sts:
            for h in range(4):
                for t0 in range(0, st.T, 512):
                    n = min(512, st.T - t0)
                    dma(st.Q["C"][h, 64:96, t0:t0 + n], fev[0][0:32, 0:n], ["fev0"], [(st.name, "QBones")], "st_ones")
                for t0 in range(0, st.NK, 512):
                    n = min(512, st.NK - t0)
                    dma(st.KC[h, 64:96, t0:t0 + n], fev[0][0:32, 0:n], ["fev0"], [(st.name, "KBones")], "st_ones")
        for st in [stp] + sts:
            for t0 in range(0, st.T, 128):
                n = min(128, st.T - t0)
                dma(st.Q["B"][:, 67:70, t0:t0 + n], ones3[:, :, 0:n], ["ones3"], [(st.name, "QBones")], "st_ones")
            for t0 in range(0, st.NK, 128):
                n = min(128, st.NK - t0)
                dma(st.KB[:, 64:67, t0:t0 + n], ones3[:, :, 0:n], ["ones3"], [(st.name, "KBones")], "st_ones")

        wsi = {"i": 0}

        def load_w(dst_ap, src_ap, npart, ncols, dstname):
            c0 = 0
            while c0 < ncols:
                c1 = min(ncols, c0 + 512)
                s = wsi["i"] % 2
                wsi["i"] += 1
                dma(wstage[s][0:npart, 0:c1 - c0], src_ap[:, c0:c1], [], [f"et{s}"], f"ld_ws{s}")
                pool("tensor_copy", [f"et{s}"], [dstname], out=dst_ap[:, c0:c1], in_=wstage[s][0:npart, 0:c1 - c0])
                c0 = c1

        def load_layer(l):
            for kc in range(8):
                load_w(Wf[:, kc, :], wf_d[l, kc * 128:(kc + 1) * 128, :], 128, NF_COLS, "Wf")
                load_w(Wt[:, kc, :], wt_d[l, kc * 128:(kc + 1) * 128, :], 128, NT_COLS, "Wt")
            for rc in range(2):
                load_w(Wuk[:, rc, :], wuk_d[l, rc * 128:(rc + 1) * 128, :], 128, 256, "Wuk")
                load_w(Wuv[:, rc, :], wuv_d[l, rc * 128:(rc + 1) * 128, :], 128, 512, "Wuv")
            for h in range(4):
                load_w(WoA[:, h, :], wo_d[l, h * 128:(h + 1) * 128, :], 128, D, "WoA")
                load_w(WoB[0:64, h, :], wo_d[l, 512 + h * 64:512 + (h + 1) * 64, :], 64, D, "WoB")
                load_w(WoC[0:64, h, :], wo_d[l, 768 + h * 64:768 + (h + 1) * 64, :], 64, D, "WoC")
            dma(gpre[:, :], npre_d[l, :].rearrange("(k p) -> p k", p=128), [], ["gpre"], "ld_small", slow=True)
            dma(gpost[:, :], npost_d[l:l + 1, :].partition_broadcast(128), [], ["gpost"], "ld_small")
            dma(kvn[:, :], kvn_d[l:l + 1, :].partition_broadcast(128), [], ["kvn"], "ld_small")
            dma(fbb[:, :], fb_d[l:l + 1, :].partition_broadcast(128), [], ["fbb"], "ld_small")
            dma(gout["A"][:, :], onorm_d[l, 0:512].rearrange("(h p) -> p h", p=128), [], ["gout"], "ld_small", slow=True)
            dma(gout["B"][:, :], onorm_d[l, 512:768].rearrange("(h p) -> p h", p=64), [], ["gout"], "ld_small", slow=True)
            dma(gout["C"][:, :], onorm_d[l, 768:1024].rearrange("(h p) -> p h", p=64), [], ["gout"], "ld_small", slow=True)

        def rstd_from_ss(ss_ap, out_ap, n, width, rd, wr):
            act(out_ap, ss_ap, AF.Ln, rd, wr, scale=1.0 / width, bias=EPS)
            act(out_ap, out_ap, AF.Exp, wr, wr, scale=-0.5)

        def rope_inplace(rw, rname, n, c0, nh, cs_key):
            x = rw[0:n, c0:c0 + nh * 32].rearrange("p (h t i) -> p h t i", h=nh, t=2)
            cosv = ropet[0:n, 0, 0:nh * 16].rearrange("p (h i) -> p h i", h=nh)
            sinv = ropet[0:n, 1, 0:nh * 16].rearrange("p (h i) -> p h i", h=nh)
            t = [ropetmp[0:n, j, 0:nh * 16].rearrange("p (h i) -> p h i", h=nh) for j in range(4)]
            x1, x2 = x[:, :, 0, :], x[:, :, 1, :]
            dve("tensor_tensor", [rname, cs_key], ["ropetmp"], out=t[0], in0=x1, in1=cosv, op=ALU.mult)
            dve("tensor_tensor", [rname, cs_key], ["ropetmp"], out=t[1], in0=x2, in1=sinv, op=ALU.mult)
            dve("tensor_tensor", [rname, cs_key], ["ropetmp"], out=t[2], in0=x1, in1=sinv, op=ALU.mult)
            dve("tensor_tensor", [rname, cs_key], ["ropetmp"], out=t[3], in0=x2, in1=cosv, op=ALU.mult)
            dve("tensor_tensor", ["ropetmp"], [rname], out=x1, in0=t[0], in1=t[1], op=ALU.subtract)
            dve("tensor_tensor", ["ropetmp"], [rname], out=x2, in0=t[2], in1=t[3], op=ALU.add)

        def rows_post(l, ri, n, st, kpos, qpos, is_new, outd, orow, cidx):
            rw, rname = rows[ri], f"rows{ri}"
            blk = kpos // 128
            sk = st.name
            if is_new:
                act(junk[0:n, 0:256], rw[0:n, CKV[0]:CKV[1]], AF.Square, [rname], ["tV", "small"], accum_out=small[0:n, 0:1])
                rstd_from_ss(small[0:n, 0:1], small[0:n, 1:2], n, 256, ["small"], ["small"])
                dve("scalar_tensor_tensor", [rname, "small", "kvn"], [rname], out=rw[0:n, CKV[0]:CKV[1]], in0=rw[0:n, CKV[0]:CKV[1]],
                    scalar=small[0:n, 1:2], in1=kvn[0:n, :], op0=ALU.mult, op1=ALU.mult)
                rope_inplace(rw, rname, n, QR[0], 4, "ropet")
                rope_inplace(rw, rname, n, KR[0], 1, "ropet")
                dve("tensor_scalar", [rname], [rname], out=rw[0:n, QR[0]:QR[1]], in0=rw[0:n, QR[0]:QR[1]], scalar1=A_SCALE, scalar2=None, op0=ALU.mult)
                dve("tensor_tensor", [rname, "fbb"], [rname], out=rw[0:n, BF[0]:BF[1]], in0=rw[0:n, BF[0]:BF[1]], in1=fbb[0:n, :], op=ALU.add)
                act(rw[0:n, BF[0]:BF[1]], rw[0:n, BF[0]:BF[1]], AF.Exp, [rname], [rname], scale=-1.0)
                act(rw[0:n, BF[0]:BF[1]], rw[0:n, BF[0]:BF[1]], AF.Ln, [rname], [rname], bias=1.0)
                dve("tensor_scalar", [rname], [rname], out=rw[0:n, BF[0]:BF[1]], in0=rw[0:n, BF[0]:BF[1]], scalar1=-1.0, scalar2=None, op0=ALU.mult)
                for nm, cr in (("ckv", CKV), ("kpe", KR), ("fk", BK), ("fv", BV), ("lf", BF), ("sk", CK), ("sv", CV)):
                    dma(outd[nm][l, orow:orow + n, :], rw[0:n, cr[0]:cr[1]], [rname], [], f"o_{rname}", eng="sp")
            pa, pb, pc = 0, 1, 2
            if is_new:
                tr(ps[pa][:, 0:n], rw[0:n, QR[0]:QR[1]], n, [rname], [PSN[pa]])
            tr(ps[pa][:, 128:128 + n], rw[0:n, CKV[0]:CKV[0] + 128], n, [rname], [PSN[pa]])
            tr(ps[pa][:, 256:256 + n], rw[0:n, CKV[0] + 128:CKV[1]], n, [rname], [PSN[pa]])
            tr(ps[pa][0:32, 384:384 + n], rw[0:n, KR[0]:KR[1]], n, [rname], [PSN[pa]])
            tr(ps[pb][:, 0:n], rw[0:n, BK[0]:BK[0] + 128], n, [rname], [PSN[pb]])
            tr(ps[pb][:, 128:128 + n], rw[0:n, BK[0] + 128:BK[1]], n, [rname], [PSN[pb]])
            tr(ps[pb][:, 256:256 + n], rw[0:n, CK[0]:CK[0] + 128], n, [rname], [PSN[pb]])
            tr(ps[pb][:, 384:384 + n], rw[0:n, CK[0] + 128:CK[1]], n, [rname], [PSN[pb]])
            tr(ps[pc][0:4, 0:n], rw[0:n, BF[0]:BF[1]], n, [rname], [PSN[pc]])
            psa = ps[pa][:, :].rearrange("p (j t) -> p j t", j=4)
            psb = ps[pb][:, :].rearrange("p (j t) -> p j t", j=4)
            j0 = 0 if is_new else 1
            act(tA[:, j0:3, 0:n], psa[:, j0:3, 0:n], AF.Copy, [PSN[pa]], ["tA"])
            act(tA[0:32, 3, 0:n], psa[0:32, 3, 0:n], AF.Copy, [PSN[pa]], ["tA"])
            dve("tensor_copy", [PSN[pb]], ["tB"], out=tB[:, :, 0:n], in_=psb[:, :, 0:n])
            dve("tensor_copy", [PSN[pc]], ["lfT"], out=lfT[0:4, 0:n], in_=ps[pc][0:4, 0:n])
            if is_new:
                for h in range(4):
                    dma(st.Q["A"][h, 64:96, qpos:qpos + n], tA[h * 32:(h + 1) * 32, 0, 0:n], ["tA"], [(sk, "QA")], "st_tA", eng="pool")
            dma(st.KPE[:, kpos:kpos + n], tA[0:32, 3, 0:n], ["tA"], [(sk, "KPE")], "st_tA", eng="pool")
            for h in range(4):
                dma(st.KB[h, 0:64, kpos:kpos + n], tB[(h % 2) * 64:(h % 2) * 64 + 64, h // 2, 0:n], ["tB"], [(sk, "KB")], "st_tB", eng="pool")
                dma(st.KC[h, 0:64, kpos:kpos + n], tB[(h % 2) * 64:(h % 2) * 64 + 64, 2 + h // 2, 0:n], ["tB"], [(sk, "KC")], "st_tB", eng="pool")
            pk, pv = 3, 4
            for cc in range(2):
                for rc in range(2):
                    mm(ps[pk][:, cc * 128:cc * 128 + n], Wuk[:, rc, cc * 128:(cc + 1) * 128], tA[:, 1 + rc, 0:n], rc == 0, rc == 1,
                       ["tA", "Wuk"], [PSN[pk]])
            for rc in range(2):
                mm(ps[pv][0:n, :], tA[:, 1 + rc, 0:n], Wuv[:, rc, :], rc == 0, rc == 1, ["tA", "Wuv"], [PSN[pv]])
            act(tK[:, :, 0:n], ps[pk][:, 0:256].rearrange("p (j t) -> p j t", j=2)[:, :, 0:n], AF.Copy, [PSN[pk]], ["tK"])
            dve("tensor_copy", [PSN[pv]], ["tV"], out=tV[0:n, :], in_=ps[pv][0:n, :])
            for h in range(4):
                dma(st.KA[h, :, kpos:kpos + n], tK[(h % 2) * 64:(h % 2) * 64 + 64, h // 2, 0:n], ["tK"], [(sk, "KA")], "st_tK", eng="pool")
            dma(st.V["A"][0:n, blk, :, :], tV[0:n, :].rearrange("p (h d) -> p h d", h=4), ["tV"], [(sk, "VA")], "st_tV", eng="pool")
            pool("tensor_copy", [rname], ["tVB"], out=tVB[0:n, :], in_=rw[0:n, BV[0]:BV[1]])
            pool("tensor_copy", [rname], ["tVC"], out=tVC[0:n, :], in_=rw[0:n, CV[0]:CV[1]])
            dma(st.V["B"][0:n, blk, :, :], tVB[0:n, :].rearrange("p (h d) -> p h d", h=4), ["tVB"], [(sk, "VB")], "st_tVB", eng="pool")
            dma(st.V["C"][0:n, blk, :, :], tVC[0:n, :].rearrange("p (h d) -> p h d", h=4), ["tVC"], [(sk, "VC")], "st_tVC", eng="pool")
            cr = carry[0:4, cidx:cidx + 1]
            dve("tensor_tensor_scan", ["onesF", "lfT", "carry"], ["Ft"], out=Ft[0:4, 0:n], data0=onesF[0:4, 0:n], data1=lfT[0:4, 0:n],
                initial=cr, op0=ALU.mult, op1=ALU.add)
            dve("tensor_copy", ["Ft"], ["carry"], out=cr, in_=Ft[0:4, n - 1:n])
            dve("tensor_copy", ["Ft"], ["Fq3"], out=Fq3[0:4, 0, 0:n], in_=Ft[0:4, 0:n])
            dve("tensor_tensor", ["Ft", "Fq3"], ["lfT"], out=Fr[0:4, 0:n], in0=Ft[0:4, 0:n], in1=Fq3[0:4, 0, 0:n], op=ALU.subtract)
            dve("tensor_copy", ["lfT"], ["Fq3"], out=Fq3[0:4, 1, 0:n], in_=Fr[0:4, 0:n])
            dve("tensor_tensor", ["lfT", "Fq3"], ["lfT"], out=Fr[0:4, 0:n], in0=Fr[0:4, 0:n], in1=Fq3[0:4, 1, 0:n], op=ALU.subtract)
            dve("tensor_copy", ["lfT"], ["Fq3"], out=Fq3[0:4, 2, 0:n], in_=Fr[0:4, 0:n])
            dve("tensor_scalar", ["Fq3"], ["Fk3"], out=Fk3[0:4, :, 0:n], in0=Fq3[0:4, :, 0:n], scalar1=-1.0, scalar2=None, op0=ALU.mult)
            dma(st.KB[:, 67:70, kpos:kpos + n], Fk3[0:4, :, 0:n], ["Fk3"], [(sk, "KB")], "st_F", eng="pool")
            if is_new:
                dma(st.Q["B"][:, 64:67, qpos:qpos + n], Fq3[0:4, :, 0:n], ["Fq3"], [(sk, "QB")], "st_F", eng="pool")

        def p1_tile(l, x_src, slabs, subs, outd, cos_d, sin_d, xkey):
            TT = sum(s[1] for s in slabs)
            for si, (r0, n, c0) in enumerate(slabs):
                xi = cnt["ld"] % 2
                cnt["ld"] += 1
                xn = f"xsl{xi}"
                dma(xsl[xi][0:n, :], x_src[r0:r0 + n, :], [xkey], [xn], f"ld_{xn}")
                act(hsl[0:n, :], xsl[xi][0:n, :], AF.Square, [xn], ["hsl", "small"], accum_out=small[0:n, 2:3])
                rstd_from_ss(small[0:n, 2:3], small[0:n, 3:4], n, D, ["small"], ["small"])
                dve("tensor_scalar", [xn, "small"], ["hsl"], out=hsl[0:n, :], in0=xsl[xi][0:n, :], scalar1=small[0:n, 3:4], scalar2=None, op0=ALU.mult)
                for half in range(2):
                    pb_ = 5 + half
                    for j in range(4):
                        kc = half * 4 + j
                        tr(ps[pb_][:, j * 128:j * 128 + n], hsl[0:n, kc * 128:(kc + 1) * 128], n, ["hsl"], [PSN[pb_]])
                    for j in range(4):
                        kc = half * 4 + j
                        if j % 2 == 0:
                            act(hT[:, kc, c0:c0 + n], ps[pb_][:, j * 128:j * 128 + n], AF.Copy, [PSN[pb_], "gpre"], ["hT"], scale=gpre[:, kc:kc + 1])
                        else:
                            dve("tensor_scalar", [PSN[pb_], "gpre"], ["hT"], out=hT[:, kc, c0:c0 + n], in0=ps[pb_][:, j * 128:j * 128 + n],
                                scalar1=gpre[:, kc:kc + 1], scalar2=None, op0=ALU.mult)
            fsubs = []
            for sub in subs:
                if fsubs and fsubs[-1][2] is sub[2] and fsubs[-1][0] + fsubs[-1][1] == sub[0] and fsubs[-1][3] + fsubs[-1][1] == sub[3]:
                    p = fsubs[-1]
                    fsubs[-1] = (p[0], p[1] + sub[1], p[2], p[3], p[4], p[5], p[6])
                else:
                    fsubs.append(tuple(sub))
            for c in range(14):
                pb_ = 5 + (c % 3)
                for kc in range(8):
                    mm(ps[pb_][:, 0:TT], Wf[:, kc, c * 128:(c + 1) * 128], hT[:, kc, 0:TT], kc == 0, kc == 7, ["Wf", "hT"], [PSN[pb_]])
                fi = c % 2
                fn = f"fev{fi}"
                if c < 6:
                    g = "ABC"[c // 2]
                    sc = (A_SCALE, B_SCALE, C_SCALE)[c // 2]
                    dve("tensor_scalar", [PSN[pb_]], [fn], out=fev[fi][:, 0:TT], in0=ps[pb_][:, 0:TT], scalar1=sc, scalar2=None, op0=ALU.mult)
                    for (sc0, n, st, qpos, orow, cidx, trow) in fsubs:
                        for hh in range(2):
                            h = (c % 2) * 2 + hh
                            dma(st.Q[g][h, 0:64, qpos:qpos + n], fev[fi][hh * 64:hh * 64 + 64, sc0:sc0 + n], [fn], [(st.name, "Q" + g)], f"st_{fn}", eng="act")
                else:
                    act(fev[fi][:, 0:TT], ps[pb_][:, 0:TT], AF.Silu, [PSN[pb_]], [fn])
                    gc = c - 6
                    for (sc0, n, st, qpos, orow, cidx, trow) in fsubs:
                        if gc < 4:
                            dma(st.G["A"][gc, :, qpos:qpos + n], fev[fi][:, sc0:sc0 + n], [fn], [(st.name, "GA")], f"st_{fn}", eng="act")
                        else:
                            g = "B" if gc < 6 else "C"
                            for hh in range(2):
                                h = (gc % 2) * 2 + hh
                                dma(st.G[g][h, :, qpos:qpos + n], fev[fi][hh * 64:hh * 64 + 64, sc0:sc0 + n], [fn], [(st.name, "G" + g)], f"st_{fn}", eng="act")
            pend = None

            def post(p):
                (sc0, n, st, qpos, orow, cidx, trow), ri = p
                dma(ropet[0:n, 0, :], cos_d[trow:trow + n, :], [], ["ropet"], "ld_rope")
                dma(ropet[0:n, 1, :], sin_d[trow:trow + n, :], [], ["ropet"], "ld_rope")
                rows_post(l, ri, n, st, st.key_off + qpos, qpos, True, outd, orow, cidx)

            for sub in subs:
                (sc0, n, st, qpos, orow, cidx, trow) = sub
                ri = cnt["ld"] % 2
                cnt["ld"] += 1
                rname = f"rows{ri}"
                groups = ((0, 420), (420, 932), (932, 1444))
                for gi, (g0, g1) in enumerate(groups):
                    pb_ = 5 + gi
                    for kc in range(8):
                        mm(ps[pb_][0:n, 0:g1 - g0], hT[:, kc, sc0:sc0 + n], Wt[:, kc, g0:g1], kc == 0, kc == 7, ["hT", "Wt"], [PSN[pb_]])
                    if gi == 1:
                        dve("tensor_copy", [PSN[pb_]], [rname], out=rows[ri][0:n, g0:g1], in_=ps[pb_][0:n, 0:g1 - g0])
                    else:
                        act(rows[ri][0:n, g0:g1], ps[pb_][0:n, 0:g1 - g0], AF.Copy, [PSN[pb_]], [rname])
                if pend is not None:
                    post(pend)
                pend = (sub, ri)
            post(pend)

        def cache_block(l, b, st, kb):
            ri = cnt["ld"] % 2
            cnt["ld"] += 1
            rname = f"rows{ri}"
            k0 = kb * 128
            for nm, cr in (("ckv", CKV), ("kpe", KR), ("fk", BK), ("fv", BV), ("lf", BF), ("sk", CK), ("sv", CV)):
                dma(rows[ri][:, cr[0]:cr[1]], c_in[nm][l, b, k0:k0 + 128, :], [], [rname], f"ld_{rname}")
            rows_post(l, ri, 128, st, k0, None, False, None, None, 1 + b)

        def attend(l, st, q0, NQ, x_src, x_dst, xrow0, xkey):
            sk = st.name
            nfull_new = NQ // 128
            blocks = []
            if st.key_off == 0:
                lastb = (q0 + NQ) // 128 - 1
                for bi in range(lastb, -1, -1):
                    m = bi - q0 // 128
                    blocks.append((bi, 128, m if m >= 0 else None))
            else:
                blocks.append((st.key_off // 128, NQ, 0))
                for bi in range(st.key_off // 128 - 1, -1, -1):
                    blocks.append((bi, 128, None))
            nb = len(blocks)
            chunks = []
            i = 0
            while i < nb:
                j = min(nb, i + KCH)
                if blocks[i][1] != 128:
                    j = i + 1
                chunks.append((i, j))
                i = j
            chunk_of = {}
            for ci, (c0, c1) in enumerate(chunks):
                for i in range(c0, c1):
                    chunk_of[i] = ci

            def head_stream(g, h, sx):
                dk, dv = DK[g], DV[g]
                gi = "ABC".index(g)
                SB = (0, 1) if sx == 0 else (4, 5)
                AUX = 2 if sx == 0 else 6
                ACC = 3 if sx == 0 else 7
                qi = 2 * sx + cnt["q%d" % sx] % 2
                cnt["q%d" % sx] += 1
                qn = f"qt{qi}"
                dma(qt[qi][0:dk, 0:NQ], st.Q[g][h, :, q0:q0 + NQ], [(sk, "Q" + g), (sk, "QBones")], [qn], f"ld_{qn}")
                kinfo = [None] * nb
                loaded = set()

                def load_chunk(ci):
                    c0, c1 = chunks[ci]
                    slot = 2 * sx + cnt["k%d" % sx] % 2
                    cnt["k%d" % sx] += 1
                    kn, vn = f"kt{slot}", f"vt{slot}"
                    blo = blocks[c1 - 1][0]
                    nblk = c1 - c0
                    klo = blo * 128
                    nkeys = sum(blocks[i][1] for i in range(c0, c1))
                    if g == "A":
                        dma(kt[slot][0:64, 0:nkeys], st.KA[h, :, klo:klo + nkeys], [(sk, "KA")], [kn], f"ld_{kn}")
                        dma(kt[slot][64:96, 0:nkeys], st.KPE[:, klo:klo + nkeys], [(sk, "KPE")], [kn], f"ld_{kn}")
                    elif g == "B":
                        dma(kt[slot][0:70, 0:nkeys], st.KB[h, :, klo:klo + nkeys], [(sk, "KB"), (sk, "KBones")], [kn], f"ld_{kn}")
                    else:
                        dma(kt[slot][0:96, 0:nkeys], st.KC[h, :, klo:klo + nkeys], [(sk, "KC"), (sk, "KBones")], [kn], f"ld_{kn}")
                    npart = 128 if blocks[c0][1] == 128 else blocks[c0][1]
                    dma(vt[slot][0:npart, 0:nblk, 0:dv], st.V[g][0:npart, blo:blo + nblk, h, :], [(sk, "V" + g)], [vn], f"ld_{vn}")
                    for i in range(c0, c1):
                        bi, nk, m = blocks[i]
                        off = (bi - blo) * 128
                        kinfo[i] = (kt[slot][0:dk, off:off + nk], vt[slot][0:nk, bi - blo, :], kn, vn)

                def need(i):
                    ci = chunk_of[i]
                    if ci not in loaded:
                        loaded.add(ci)
                        load_chunk(ci)

                def cst(i):
                    m = blocks[i][2]
                    return 128 * m if (m is not None and st.key_off == 0) else 0

                def qk(i, bank):
                    bi, nk, m = blocks[i]
                    kap, vap, kn, vn = kinfo[i]
                    c0 = cst(i)
                    mm(ps[bank][0:nk, c0:NQ], kap, qt[qi][0:dk, c0:NQ], True, m is None, [kn, qn], [PSN[bank]])
                    if m is not None:
                        mm(ps[bank][0:nk, c0:NQ], ident_b[0:nk, 0:nk], masks[0:nk, gi * 4 + m, c0:NQ], False, True,
                           ["ident_b", "masks"], [PSN[bank]])

                if g in "AB":
                    DEN = AUX
                    need(0)
                    qk(0, SB[0])
                    for i in range(nb + 1):
                        if i >= 1:
                            j = i - 1
                            bi, nk, m = blocks[j]
                            kap, vap, kn, vn = kinfo[j]
                            pj = 2 * sx + j % 2
                            c0 = cst(j)
                            mm(ps[ACC][:, c0:NQ], vap, pt[pj][0:nk, c0:NQ], j == 0, j == nb - 1, [vn, f"pt{pj}"], [PSN[ACC]], skip=True)
                            mm(ps[DEN][:, c0:NQ], ones_b[0:nk, 0:128], pt[pj][0:nk, c0:NQ], j == 0, j == nb - 1, ["ones_b", f"pt{pj}"], [PSN[DEN]], skip=True)
                        yield
                        if i < nb:
                            bi, nk, m = blocks[i]
                            bank = SB[i % 2]
                            pi = 2 * sx + i % 2
                            c0 = cst(i)
                            act(pt[pi][0:nk, c0:NQ], ps[bank][0:nk, c0:NQ], AF.Exp, [PSN[bank]], [f"pt{pi}"])
                        yield
                        if i + 1 < nb:
                            need(i + 1)
                            qk(i + 1, SB[(i + 1) % 2])
                        yield
                        yield
                    act(rbt[0:dv, 0:NQ], ps[DEN][0:dv, 0:NQ], AF.Ln, [PSN[DEN]], ["sqt"])
                    act(rbt[0:dv, 0:NQ], rbt[0:dv, 0:NQ], AF.Exp, ["sqt"], ["sqt"], scale=-1.0)
                    dve("tensor_tensor", [PSN[ACC], "sqt"], [("OC" if g == "C" else "Oall")], out=Og[g][0:dv, h, 0:NQ], in0=ps[ACC][0:dv, 0:NQ], in1=rbt[0:dv, 0:NQ], op=ALU.mult)
                else:
                    RB = AUX
                    def act1(t):
                        bi, nk, m = blocks[t]
                        zb, s3 = SB[t % 2], t % 3
                        c0 = cst(t)
                        act(et[s3][0:nk, c0:NQ], ps[zb][0:nk, c0:NQ], AF.Exp, [PSN[zb]], [f"et{s3}"])
                        act(spt[s3][0:nk, c0:NQ], et[s3][0:nk, c0:NQ], AF.Ln, [f"et{s3}"], [f"spt{s3}"], bias=1.0)

                    def pv(t):
                        bi, nk, m = blocks[t]
                        kap, vap, kn, vn = kinfo[t]
                        s2 = t % 2
                        c0 = cst(t)
                        mm(ps[ACC][:, c0:NQ], vap, wt_[s2][0:nk, c0:NQ], t == 0, t == nb - 1, [vn, f"wt{s2}"], [PSN[ACC]], skip=True)

                    need(0)
                    qk(0, SB[0])
                    if nb > 1:
                        need(1)
                        qk(1, SB[1])
                    act1(0)
                    for t in range(nb + 1):
                        if t < nb:
                            bi, nk, m = blocks[t]
                            s2, s3 = t % 2, t % 3
                            c0 = cst(t)
                            mm(ps[RB][:, c0:NQ], trin_b[0:nk, 0, :], spt[s3][0:nk, c0:NQ], t == 0, False, ["trin_b", f"spt{s3}"], [PSN[RB]], skip=True)
                        if t >= 1:
                            pv(t - 1)
                        yield
                        if t < nb:
                            act(argt[s2][0:nk, c0:NQ], ps[RB][0:nk, c0:NQ], AF.Exp, [PSN[RB]], [f"argt{s2}"], scale=-1.0)
                            if t + 1 < nb:
                                act1(t + 1)
                        yield
                        if t + 2 < nb:
                            need(t + 2)
                            qk(t + 2, SB[t % 2])
                        yield
                        if t < nb:
                            mm(ps[RB][:, c0:NQ], trin_b[0:nk, 1, :], spt[s3][0:nk, c0:NQ], False, t == nb - 1, ["trin_b", f"spt{s3}"], [PSN[RB]], skip=True)
                            dve("tensor_tensor", [f"et{s3}", f"argt{s2}"], [f"wt{s2}"], out=wt_[s2][0:nk, c0:NQ], in0=et[s3][0:nk, c0:NQ],
                                in1=argt[s2][0:nk, c0:NQ], op=ALU.mult)
                        yield
                    act(Og[g][0:dv, h, 0:NQ], ps[ACC][0:dv, 0:NQ], AF.Copy, [PSN[ACC]], [("OC" if g == "C" else "Oall")])

            def run_pair(gens):
                alive = list(gens)
                while alive:
                    for gen in list(alive):
                        try:
                            next(gen)
                        except StopIteration:
                            alive.remove(gen)

            def merge(g):
                dv = DV[g]
                dma(gtile[g][0:dv, :, 0:NQ], st.G[g][:, :, q0:q0 + NQ].rearrange("h p t -> p h t"), [(sk, "G" + g)], ["gtall"], "ld_gt")
                width = 4 * dv
                for h in range(4):
                    act(sqt[0:dv, 0:NQ], Og[g][0:dv, h, 0:NQ], AF.Square, [("OC" if g == "C" else "Oall")], ["sqt"])
                    mm(ps[2][:, 0:NQ], ones_f[0:dv, 0:128], sqt[0:dv, 0:NQ], h == 0, h == 3, ["ones_f", "sqt"], [PSN[2]])
                act(lnt[:, 0:NQ], ps[2][:, 0:NQ], AF.Ln, [PSN[2]], ["lnt"], scale=1.0 / width, bias=EPS)
                act(rst[:, 0:NQ], lnt[:, 0:NQ], AF.Exp, ["lnt"], ["lnt"], scale=-0.5)
                for h in range(4):
                    dve("scalar_tensor_tensor", [("OC" if g == "C" else "Oall"), "gout", "lnt"], ["sqt"], out=tmpm[0:dv, 0:NQ], in0=Og[g][0:dv, h, 0:NQ],
                        scalar=gout[g][0:dv, h:h + 1], in1=rst[0:dv, 0:NQ], op0=ALU.mult, op1=ALU.mult)
                    dve("tensor_tensor", ["sqt", "gtall"], ["M" + g], out=Mg[g][0:dv, h, 0:NQ], in0=tmpm[0:dv, 0:NQ], in1=gtile[g][0:dv, h, 0:NQ], op=ALU.mult)

            for h in range(4):
                run_pair([head_stream("A", h, 0), head_stream("C", h, 1)])
            merge("A")
            merge("C")
            run_pair([head_stream("B", 0, 0), head_stream("B", 1, 1)])
            run_pair([head_stream("B", 2, 0), head_stream("B", 3, 1)])
            merge("B")
            for s0 in range(0, NQ, 128):
                n = min(128, NQ - s0)
                dma(xres[0:n, :], x_src[xrow0 + s0:xrow0 + s0 + n, :], [xkey], ["xsl0"], "ld_xres")
                for half in range(2):
                    bank = 4 + half
                    k = 0
                    for g in "ABC":
                        dv = DV[g]
                        Wo = {"A": WoA, "B": WoB, "C": WoC}[g]
                        for h in range(4):
                            mm(ps[bank][0:n, :], Mg[g][:, h, s0:s0 + n], Wo[:, h, half * 512:(half + 1) * 512], k == 0, k == 11,
                               ["M" + g, "Wo" + g], [PSN[bank]])
                            k += 1
                    act(junk[0:n, 0:512], ps[bank][0:n, :], AF.Square, [PSN[bank]], ["tV", "small"], accum_out=small[0:n, 4 + half:5 + half])
                dve("tensor_tensor", ["small"], ["small"], out=small[0:n, 6:7], in0=small[0:n, 4:5], in1=small[0:n, 5:6], op=ALU.add)
                rstd_from_ss(small[0:n, 6:7], small[0:n, 7:8], n, D, ["small"], ["small"])
                for half in range(2):
                    bank = 4 + half
                    cs_ = slice(half * 512, (half + 1) * 512)
                    dve("scalar_tensor_tensor", [PSN[bank], "small", "gpost"], ["xsl1"], out=xnew[0:n, cs_], in0=ps[bank][0:n, :],
                        scalar=small[0:n, 7:8], in1=gpost[0:n, cs_], op0=ALU.mult, op1=ALU.mult)
                dve("tensor_tensor", ["xsl1", "xsl0"], ["xsl1"], out=xnew[0:n, :], in0=xnew[0:n, :], in1=xres[0:n, :], op=ALU.add)
                dma(x_dst[xrow0 + s0:xrow0 + s0 + n, :], xnew[0:n, :], ["xsl1"], [xkey], "st_xnew", eng="pool")

        for l in range(DEPTH):
            load_layer(l)
            pool("memset", [], ["carry"], carry[:, :], 0.0)
            xsrc_p = xp if l == 0 else xbuf_p
            xdst_p = yp if l == DEPTH - 1 else xbuf_p
            xsrc_s = xs if l == 0 else xbuf_s
            xdst_s = ys if l == DEPTH - 1 else xbuf_s
            for t0 in range(0, SEQ, 512):
                TT = min(512, SEQ - t0)
                slabs = [(t0 + j * 128, 128, j * 128) for j in range(TT // 128)]
                subs = [(j * 128, 128, stp, t0 + j * 128, t0 + j * 128, 0, t0 + j * 128) for j in range(TT // 128)]
                p1_tile(l, xsrc_p, slabs, subs, outp, cosp_d, sinp_d, "xbufP")
            S.barrier("sp")
            slabs = [(0, TS, 0)]
            subs = [(b * DS, DS, sts[b], 0, b * DS, 1 + b, 0) for b in range(NSB)]
            tasks = [(lambda b=b, kb=kb: cache_block(l, b, sts[b], kb)) for b in range(NSB) for kb in range(PAST // 128)]
            tasks.append(lambda: p1_tile(l, xsrc_s, slabs, subs, outs, coss_d, sins_d, "xbufS"))
            nqt = SEQ // NQP
            per = -(-len(tasks) // nqt)
            ti = 0
            for q0 in range(0, SEQ, NQP):
                attend(l, stp, q0, NQP, xsrc_p, xdst_p, q0, "xbufP")
                for _ in range(per):
                    if ti < len(tasks):
                        tasks[ti]()
                        ti += 1
            while ti < len(tasks):
                tasks[ti]()
                ti += 1
            S.barrier("sp")
            for b in range(NSB):
                attend(l, sts[b], 0, DS, xsrc_s, xdst_s, b * DS, "xbufS")
            S.barrier("sp")

        S.build()
        sems = {}
        for e in S.ENGS:
            sems[("E", e)] = es.enter_context(nc.semaphore("s_" + e))
        for i, k in enumerate(S.dma_keys):
            sems[("D", k)] = es.enter_context(nc.semaphore(f"d{i}"))
        with nc.Block() as block:
            @block.tensor
            def _(eng):
                S.emit("pe", eng, sems)

            @block.scalar
            def _(eng):
                S.emit("act", eng, sems)

            @block.vector
            def _(eng):
                S.emit("dve", eng, sems)

            @block.gpsimd
            def _(eng):
                S.emit("pool", eng, sems)

            @block.sync
            def _(eng):
                S.emit("sp", eng, sems)
    nc._sched_stats = {e: len(S.ops[e]) for e in S.ENGS}
    return nc


def make_consts(SEQ, PAST):
    idx = np.arange(128)
    ident = np.eye(128, dtype=np.float32)
    trin = np.stack([(idx[:, None] >= idx[None, :]), (idx[:, None] < idx[None, :])]).astype(np.float32)
    kl = np.arange(128)[:, None]
    ql = np.arange(512)[None, :]
    masks = np.zeros((12, 128, 512), np.float32)
    for m in range(4):
        kp = 128 * m + kl
        masks[0 + m] = np.where((kp // 64) <= (ql // 64), 0.0, NEG)
        masks[4 + m] = np.where(kp <= ql, 0.0, NEG)
        masks[8 + m] = np.where(kp < ql, 0.0, NEG)
    half = 16
    inv = (10000.0 ** (-np.arange(half, dtype=np.float32) / half)).astype(np.float32)

    def tables(pos):
        ang = pos.astype(np.float32)[:, None] * inv[None, :]
        c = np.cos(ang).astype(np.float32)
        s = np.sin(ang).astype(np.float32)
        return np.tile(c, (1, 4)), np.tile(s, (1, 4))

    cosp, sinp = tables(np.arange(SEQ))
    coss, sins = tables(PAST + np.arange(DS))
    return dict(ident=ident, trin=trin, masks=masks, cosp=cosp, sinp=sinp, coss=coss, sins=sins)


def split_w_in(w_in):
    sizes = (256, 128, 256, 32, 256, 256, 256, 4, 256, 256, 256, 1024)
    offs = np.cumsum((0,) + sizes)
    seg = lambda i: w_in[:, :, offs[i]:offs[i + 1]]
    a_qn, a_qr, a_ckv, a_kr, b_q, b_k, b_v, b_f, c_q, c_k, c_v, gate = (seg(i) for i in range(12))
    wf = np.concatenate([a_qn, b_q, c_q, gate], axis=-1)
    wt = np.concatenate([a_qr, a_ckv, a_kr, b_f, b_k, b_v, c_k, c_v], axis=-1)
    return np.ascontiguousarray(wf), np.ascontiguousarray(wt)


_NC_CACHE = {}


def run(inputs, SEQ, PAST, DEPTH, n_cores, NSB):
    f = lambda a: np.ascontiguousarray(np.asarray(a, dtype=np.float32))
    key = (SEQ, PAST, DEPTH, NSB)
    if key not in _NC_CACHE:
        _NC_CACHE[key] = build(SEQ, PAST, DEPTH, NSB)
    nc = _NC_CACHE[key]
    consts = make_consts(SEQ, PAST)
    wf, wt = split_w_in(f(inputs["w_in"]))
    common = dict(wf=wf, wt=wt, wuk=f(inputs["mla_w_uk"]).reshape(DEPTH, 256, 256), wuv=f(inputs["mla_w_uv"]).reshape(DEPTH, 256, 512),
                  wo=f(inputs["w_out"]), npre=f(inputs["norm_pre"]), npost=f(inputs["norm_post"]), kvn=f(inputs["mla_kv_norm"]),
                  fb=f(inputs["fox_forget_bias"]), onorm=f(inputs["out_norm"]), **consts)
    xp, xs = f(inputs["x_prompt"]), f(inputs["x_sample"])
    nseq = xp.shape[0]
    cpp = n_cores // nseq
    in_maps = []
    for c in range(n_cores):
        b0 = c * NSB
        m = dict(common)
        m["xp"] = xp[c // cpp]
        m["xs"] = np.ascontiguousarray(xs[b0:b0 + NSB].reshape(NSB * DS, D))
        m["c_ckv"] = f(inputs["cache_mla_ckv"][:, b0:b0 + NSB])
        m["c_kpe"] = f(inputs["cache_mla_kpe"][:, b0:b0 + NSB])
        m["c_fk"] = f(inputs["cache_fox_k"][:, b0:b0 + NSB]).reshape(DEPTH, NSB, PAST, 256)
        m["c_fv"] = f(inputs["cache_fox_v"][:, b0:b0 + NSB]).reshape(DEPTH, NSB, PAST, 256)
        m["c_lf"] = f(inputs["cache_fox_logf"][:, b0:b0 + NSB])
        m["c_sk"] = f(inputs["cache_sb_k"][:, b0:b0 + NSB]).reshape(DEPTH, NSB, PAST, 256)
        m["c_sv"] = f(inputs["cache_sb_v"][:, b0:b0 + NSB]).reshape(DEPTH, NSB, PAST, 256)
        in_maps.append(m)
    res = run_bass_kernel_spmd(nc, in_maps, core_ids=list(range(n_cores)))
    R = res.results
    pc = [R[i * cpp] for i in range(nseq)]
    y_p = np.stack([r["yp"] for r in pc], 0)
    y_s = np.concatenate([r["ys"].reshape(NSB, DS, D) for r in R], 0)
    outs = [y_p, y_s]
    shp = {"ckv": (256,), "kpe": (32,), "fk": (4, 64), "fv": (4, 64), "lf": (4,), "sk": (4, 64), "sv": (4, 64)}
    for nm in ("ckv", "kpe", "fk", "fv", "lf", "sk", "sv"):
        a = np.stack([r["p_" + nm] for r in pc], 1)
        outs.append(a.reshape(a.shape[:3] + shp[nm]))
    for nm in ("ckv", "kpe", "fk", "fv", "lf", "sk", "sv"):
        a = np.concatenate([r["s_" + nm].reshape(DEPTH, NSB, DS, -1) for r in R], 1)
        outs.append(a.reshape(a.shape[:3] + shp[nm]))
    return tuple(np.ascontiguousarray(o, dtype=np.float32) for o in outs)


def kernel(**inputs):
    return run(inputs, SEQ=8192, PAST=2048, DEPTH=4, n_cores=8, NSB=4)
```
